# Optimizing a Trainium2 kernel written in Bass

```python
import jax, jax.numpy as jnp
from jax import lax
import numpy as np

D_MODEL = 1024
BATCH = 8
SEQ = 4096
DEPTH = 2

N_MIXERS = 2
D_FF = 2816
CONV_WIDTH = 31
ATTN_GROUPS = ((128, 1), (512, 4), (2048, 16))
N_GROUPS = len(ATTN_GROUPS)
HEADS_PER_GROUP = 8
HEAD_DIM = 128
QKV_COLS = N_GROUPS * 3 * HEADS_PER_GROUP * HEAD_DIM
ATTN_OUT = HEADS_PER_GROUP * HEAD_DIM
ATTN_BLOCK = 128
NORM_EPS = 1e-6
N_CONV_LAYERS = (DEPTH + N_MIXERS - 1) // N_MIXERS
N_ATTN_LAYERS = DEPTH // N_MIXERS

kernel_name = "hybrid_conformer_conv_dilated_attn_macaron"


def rmsnorm(x, g):
    xf = x.astype(jnp.float32)
    y = xf * lax.rsqrt(jnp.mean(xf * xf, axis=-1, keepdims=True) + NORM_EPS)
    return (y * g.astype(jnp.float32)).astype(x.dtype)


def swiglu_ffn(h, w_in, w_out):
    gate, up = jnp.split(h @ w_in, 2, axis=-1)
    return (jax.nn.silu(gate) * up) @ w_out


def conformer_conv_module(h, w_pw1, b_pw1, w_dw, b_dw, norm_g, w_pw2, b_pw2):
    a, gate = jnp.split(h @ w_pw1 + b_pw1, 2, axis=-1)
    u = a * jax.nn.sigmoid(gate)
    u = lax.conv_general_dilated(
        u, w_dw[:, None, :].astype(u.dtype), window_strides=(1,),
        padding=((CONV_WIDTH - 1, 0),),
        dimension_numbers=("NWC", "WIO", "NWC"),
        feature_group_count=D_MODEL) + b_dw
    u = jax.nn.silu(rmsnorm(u, norm_g))
    return u @ w_pw2 + b_pw2


def _band_mask(n_blocks, steps):
    i = jnp.arange(ATTN_BLOCK)[:, None]
    j = jnp.arange(2 * ATTN_BLOCK)[None, :]
    diff = ATTN_BLOCK + i - j
    band = (diff >= 0) & (diff <= steps)
    key_pos = jnp.arange(n_blocks)[:, None, None] * ATTN_BLOCK - ATTN_BLOCK + j[None]
    return band[None] & (key_pos >= 0)


def dilated_window_attention(q, k, v, window, dilation):
    B, S, H, E = q.shape
    steps = window // dilation
    L = S // dilation
    nb = -(-L // ATTN_BLOCK)
    Lp = nb * ATTN_BLOCK

    def to_streams(t):
        t = t.reshape(B, L, dilation, H, E).transpose(0, 2, 1, 3, 4)
        t = jnp.pad(t, ((0, 0), (0, 0), (0, Lp - L), (0, 0), (0, 0)))
        return t.reshape(B, dilation, nb, ATTN_BLOCK, H, E)

    def with_prev(t):
        prev = jnp.concatenate([jnp.zeros_like(t[:, :, :1]), t[:, :, :-1]], axis=2)
        return jnp.concatenate([prev, t], axis=3)

    qb = to_streams(q)
    kw = with_prev(to_streams(k))
    vw = with_prev(to_streams(v))
    s = jnp.einsum("bdnqhe,bdnkhe->bdnhqk", qb, kw).astype(jnp.float32) * (HEAD_DIM ** -0.5)
    valid = _band_mask(nb, steps)[None, None, :, None]
    s = jnp.where(valid, s, -jnp.inf)
    m = jnp.max(s, axis=-1, keepdims=True)
    p = jnp.exp(s - m)
    l = jnp.sum(p, axis=-1, keepdims=True)
    o = jnp.einsum("bdnhqk,bdnkhe->bdnqhe", p / l, vw.astype(jnp.float32))
    lse = (m + jnp.log(l))[..., 0]
    o = o.reshape(B, dilation, Lp, H, E)[:, :, :L].transpose(0, 2, 1, 3, 4).reshape(B, S, H, E)
    lse = lse.transpose(0, 1, 2, 4, 3).reshape(B, dilation, Lp, H)[:, :, :L]
    lse = lse.transpose(0, 2, 1, 3).reshape(B, S, H)
    return o, lse


def dilated_attention_mixer(h, w_qkv, q_norm, k_norm, w_o):
    B, S, _ = h.shape
    qkv = (h @ w_qkv).reshape(B, S, N_GROUPS, 3, HEADS_PER_GROUP, HEAD_DIM)
    outs, lses = [], []
    for g, (window, dilation) in enumerate(ATTN_GROUPS):
        q = rmsnorm(qkv[:, :, g, 0], q_norm[g])
        k = rmsnorm(qkv[:, :, g, 1], k_norm[g])
        o, lse = dilated_window_attention(q, k, qkv[:, :, g, 2], window, dilation)
        outs.append(o)
        lses.append(lse)
    wts = jax.nn.softmax(jnp.stack(lses), axis=0)
    o = jnp.einsum("gbsh,gbshe->bshe", wts, jnp.stack(outs))
    return o.reshape(B, S, ATTN_OUT).astype(h.dtype) @ w_o


def setup_inputs(seed: int = 0) -> dict:
    key = jax.random.key(seed)
    ks = jax.random.split(key, 16)
    nrm = lambda k, shape, scale: jax.random.normal(k, shape, jnp.float32) * scale
    return {
        "x": nrm(ks[0], (BATCH, SEQ, D_MODEL), 1.0),
        "norm_g": 1.0 + nrm(ks[1], (DEPTH, 3, D_MODEL), 0.02),
        "ffn_w_in": nrm(ks[2], (DEPTH, 2, D_MODEL, 2 * D_FF), D_MODEL ** -0.5),
        "ffn_w_out": nrm(ks[3], (DEPTH, 2, D_FF, D_MODEL), D_FF ** -0.5),
        "conv_w_pw1": nrm(ks[4], (N_CONV_LAYERS, D_MODEL, 2 * D_MODEL), D_MODEL ** -0.5),
        "conv_b_pw1": nrm(ks[5], (N_CONV_LAYERS, 2 * D_MODEL), 0.01),
        "conv_w_dw": nrm(ks[6], (N_CONV_LAYERS, CONV_WIDTH, D_MODEL), CONV_WIDTH ** -0.5),
        "conv_b_dw": nrm(ks[7], (N_CONV_LAYERS, D_MODEL), 0.01),
        "conv_norm_g": 1.0 + nrm(ks[8], (N_CONV_LAYERS, D_MODEL), 0.02),
        "conv_w_pw2": nrm(ks[9], (N_CONV_LAYERS, D_MODEL, D_MODEL), D_MODEL ** -0.5),
        "conv_b_pw2": nrm(ks[10], (N_CONV_LAYERS, D_MODEL), 0.01),
        "attn_w_qkv": nrm(ks[11], (N_ATTN_LAYERS, D_MODEL, QKV_COLS), D_MODEL ** -0.5),
        "attn_q_norm": 1.0 + nrm(ks[12], (N_ATTN_LAYERS, N_GROUPS, HEAD_DIM), 0.02),
        "attn_k_norm": 1.0 + nrm(ks[13], (N_ATTN_LAYERS, N_GROUPS, HEAD_DIM), 0.02),
        "attn_w_o": nrm(ks[14], (N_ATTN_LAYERS, ATTN_OUT, D_MODEL), ATTN_OUT ** -0.5),
    }


def reference(x, norm_g, ffn_w_in, ffn_w_out, conv_w_pw1, conv_b_pw1, conv_w_dw, conv_b_dw,
              conv_norm_g, conv_w_pw2, conv_b_pw2, attn_w_qkv, attn_q_norm, attn_k_norm, attn_w_o):
    for layer in range(DEPTH):
        x = x + 0.5 * swiglu_ffn(rmsnorm(x, norm_g[layer, 0]), ffn_w_in[layer, 0], ffn_w_out[layer, 0])
        h = rmsnorm(x, norm_g[layer, 1])
        idx = layer // N_MIXERS
        if layer % N_MIXERS == 0:
            mix = conformer_conv_module(h, conv_w_pw1[idx], conv_b_pw1[idx], conv_w_dw[idx], conv_b_dw[idx],
                                        conv_norm_g[idx], conv_w_pw2[idx], conv_b_pw2[idx])
        else:
            mix = dilated_attention_mixer(h, attn_w_qkv[idx], attn_q_norm[idx], attn_k_norm[idx], attn_w_o[idx])
        x = x + mix
        x = x + 0.5 * swiglu_ffn(rmsnorm(x, norm_g[layer, 2]), ffn_w_in[layer, 1], ffn_w_out[layer, 1])
    return x
```

```python
import contextlib
import numpy as np
import concourse.bass as bass
import concourse.mybir as mybir
from concourse.bass_utils import run_bass_kernel_spmd

F32 = mybir.dt.float32
BF16 = mybir.dt.bfloat16
AF = mybir.ActivationFunctionType
ALU = mybir.AluOpType

P = 128
D = 1024
DC = 8
FF = 2816
FC = 22
T = 512
NHEAD = 8
DIL = (1, 4, 16)
EPS = 1e-6
CW = 31
NEG = -30000.0
NVR = 42
NA = 132
NB = 48
SAME_ENGINE_SYNC = True
STQ = "act"


class Res:
    __slots__ = ("w", "r")

    def __init__(self):
        self.w = None
        self.r = {}


def mkres(n):
    return [Res() for _ in range(n)]


class Chan:
    def __init__(self, nc, name):
        self.sem = nc.alloc_semaphore(name=name)
        self.n = 0


class Prog:
    ENG = ("pe", "act", "dve", "pool", "sp")

    def __init__(self, nc):
        self.nc = nc
        self.sem = {e: nc.alloc_semaphore(name="cs_" + e) for e in self.ENG if e != "sp"}
        self.cnt = {e: 0 for e in self.ENG}
        self.q = {e: [] for e in self.ENG}
        self.seen = {e: {} for e in self.ENG}
        self.chans = []
        self.nops = 0

    def chan(self, name):
        c = Chan(self.nc, f"{name}_{len(self.chans)}")
        self.chans.append(c)
        return c

    def op(self, eng, fn, reads=(), writes=(), inc=True, chan=None, chain=False):
        evs = {}

        def need(ev):
            if ev is None:
                return
            s, v = ev
            k = s.num
            if k not in evs or evs[k][1] < v:
                evs[k] = (s, v)

        for r in reads:
            need(r.w)
        for w in writes:
            need(w.w)
            for ev in w.r.values():
                need(ev)
        if chan is not None and chan.n > 0 and not chain:
            need((chan.sem, chan.n * 16))
        if chain:
            evs = {}
        waits = []
        seen = self.seen[eng]
        for k, (s, v) in evs.items():
            if eng != "sp" and s is self.sem[eng]:
                if eng == "pe" or not SAME_ENGINE_SYNC:
                    continue
            if seen.get(k, 0) >= v:
                continue
            seen[k] = v
            waits.append((s, v))
        if chan is not None:
            chan.n += 1
            ev = (chan.sem, chan.n * 16)
            incs = (chan.sem, 16)
        elif inc:
            self.cnt[eng] += 1
            ev = (self.sem[eng], self.cnt[eng])
            incs = (self.sem[eng], 1)
        else:
            assert eng == "pe"
            ev = (self.sem[eng], self.cnt[eng] + 1)
            incs = None
        self.q[eng].append((waits, fn, incs))
        self.nops += 1
        k = ev[0].num
        for r in reads:
            if k not in r.r or r.r[k][1] < ev[1]:
                r.r[k] = ev
        for w in writes:
            w.w = ev
            w.r = {}
        return ev

    def barrier(self):
        evs = []
        for e in self.ENG:
            if e != "sp" and self.cnt[e] > 0:
                evs.append((self.sem[e], self.cnt[e]))
        for c in self.chans:
            if c.n > 0:
                evs.append((c.sem, c.n * 16))
        for e in self.ENG:
            waits = []
            for s, v in evs:
                if e != "sp" and s is self.sem[e]:
                    continue
                if self.seen[e].get(s.num, 0) >= v:
                    continue
                self.seen[e][s.num] = v
                waits.append((s, v))
            self.q[e].append((waits, None, None))

    def emit(self):
        nc = self.nc

        def mk(e):
            items = self.q[e]

            def body(engine):
                for waits, fn, incs in items:
                    for s, v in waits:
                        engine.wait_ge(s, v)
                    if fn is not None:
                        ins = fn(engine)
                        if incs is not None:
                            ins.then_inc(incs[0], incs[1])
            return body

        with nc.Block() as block:
            block.tensor(mk("pe"))
            block.scalar(mk("act"))
            block.vector(mk("dve"))
            block.gpsimd(mk("pool"))
            block.sync(mk("sp"))
        self.q = {e: [] for e in self.ENG}


class Bank:
    def __init__(self, t):
        self.t = t
        self.res = Res()


class WStream:
    def __init__(self, K, kind, slots, plan):
        self.K = K
        self.kind = kind
        self.slots = slots
        self.sres = mkres(len(slots))
        self.chs = [K.pg.chan(f"w{kind}{i}") for i in range(len(slots))]
        self.plan = plan
        self.pos = 0
        self.filled = 0

    def acquire(self):
        n = len(self.slots)
        lim = min(len(self.plan), self.pos + n)
        while self.filled < lim:
            self._fill(self.filled)
            self.filled += 1
        k = self.pos
        self.pos += 1
        return self.slots[k % n], self.sres[k % n]

    def _fill(self, k):
        K = self.K
        pg = K.pg
        n = len(self.slots)
        s = k % n
        idx = self.plan[k]
        slot, sres, ch = self.slots[s], self.sres[s], self.chs[s]
        conv = K.convA if self.kind == "A" else K.convB
        scr = K.wA if self.kind == "A" else K.wB
        scr_res = K.wA_res if self.kind == "A" else K.wB_res
        nel = 2048 if self.kind == "A" else K.srcB(idx)[1] * 128
        if idx in conv:
            K.flush_stores(0)
            pg.op("sp", lambda e, o=slot[:, 0:nel], i=scr[idx][:, 0:nel]: e.dma_start(out=o, in_=i),
                  reads=[scr_res[idx]], writes=[sres], chan=ch)
            return
        conv.add(idx)
        si = K.stage_i
        K.stage_i += 1
        st, stres, stch = K.stageF[si % 2], K.stage_res[si % 2], K.stage_ch[si % 2]
        if self.kind == "A":
            src, ca, cb = K.srcA(idx)
            stv = st[:, 0:2048].rearrange("p (a c) -> p a c", c=256)
            sv = src.rearrange("(a p) c -> p a c", p=P)
            pg.op("sp", lambda e, o=stv[:, :, 0:128], i=sv[:, :, ca:ca + 128]: e.dma_start(out=o, in_=i),
                  writes=[stres], chan=stch)
            pg.op("sp", lambda e, o=stv[:, :, 128:256], i=sv[:, :, cb:cb + 128]: e.dma_start(out=o, in_=i),
                  writes=[stres], chan=stch, chain=True)
            nel = 2048
        else:
            src, kc, c0 = K.srcB(idx)
            nel = kc * 128
            stv = st[:, 0:nel].rearrange("p (a c) -> p a c", c=128)
            sv = src.rearrange("(a p) c -> p a c", p=P)
            pg.op("sp", lambda e, o=stv, i=sv[:, :, c0:c0 + 128]: e.dma_start(out=o, in_=i),
                  writes=[stres], chan=stch)
        ce = ("pool", "dve", "act", "dve", "act")[si % 5]
        if ce == "act":
            pg.op("act", lambda e, o=slot[:, 0:nel], i=st[:, 0:nel]: e.activation(out=o, in_=i, func=AF.Identity),
                  reads=[stres], writes=[sres])
        else:
            pg.op(ce, lambda e, o=slot[:, 0:nel], i=st[:, 0:nel]: e.tensor_copy(out=o, in_=i),
                  reads=[stres], writes=[sres])
        K.flush_stores(1)
        K.pending_stores.append(lambda: pg.op(STQ, lambda e, o=scr[idx][:, 0:nel], i=slot[:, 0:nel]: e.dma_start(out=o, in_=i),
                                              reads=[sres], writes=[scr_res[idx]], chan=K.cst_ch[si % 2]))


class Kern:
    def __init__(self, S=4096, upto="full"):
        self.S = S
        self.NT = S // T
        self.upto = upto
        self.nc = bass.Bass("TRN2", target_bir_lowering=False)
        self.pg = Prog(self.nc)
        self.convA = set()
        self.convB = set()
        self.stage_i = 0
        self.pending_stores = []
        self.qstores = []
        self.pend_b = None
        self.qk_i = 0
        self.build()

    def srcA(self, idx):
        if idx < 88:
            li, j = divmod(idx, 22)
            return self.d_win[li], j * 128, FF + j * 128
        if idx < 96:
            c = idx - 88
            return self.d_pw1, c * 128, D + c * 128
        if idx < 120:
            g, hh = divmod(idx - 96, 8)
            return self.d_qkv, g * 3072 + hh * 128, g * 3072 + 1024 + hh * 128
        g, pr = divmod(idx - 120, 4)
        c0 = g * 3072 + 2048 + pr * 256
        return self.d_qkv, c0, c0 + 128

    def srcB(self, idx):
        if idx < 32:
            li, dc = divmod(idx, 8)
            return self.d_wout[li], FC, dc * 128
        if idx < 40:
            return self.d_pw2, DC, (idx - 32) * 128
        return self.d_wo, DC, (idx - 40) * 128

    def flush_stores(self, keep):
        while len(self.pending_stores) > keep:
            self.pending_stores.pop(0)()

    def bank(self):
        b = self.banks[self.bank_i % 7]
        self.bank_i += 1
        return b

    def dram_in(self, name, shape, dt=F32):
        return self.nc.dram_tensor(name, list(shape), dt, kind="ExternalInput").ap()

    def build(self):
        nc, pg, S, NT = self.nc, self.pg, self.S, self.NT
        op = pg.op
        self.d_x = self.dram_in("x", [S, D])
        d_vrow = self.dram_in("vrow", [NVR, D])
        d_qkn = self.dram_in("qkn", [6, 128])
        d_ident = self.dram_in("ident", [P, P])
        d_mask = self.dram_in("maskneg", [P, 256])
        win = self.dram_in("ffn_w_in", [4, D, 2 * FF])
        wout = self.dram_in("ffn_w_out", [4, FF, D])
        self.d_win = [win[i] for i in range(4)]
        self.d_wout = [wout[i] for i in range(4)]
        self.d_pw1 = self.dram_in("conv_w_pw1", [D, 2 * D])
        self.d_pw2 = self.dram_in("conv_w_pw2", [D, D])
        self.d_qkv = self.dram_in("attn_w_qkv", [D, 9216])
        self.d_wo = self.dram_in("attn_w_o", [D, D])
        self.d_out = nc.dram_tensor("out", [S, D], F32, kind="ExternalOutput").ap()
        wA = nc.dram_tensor("wA", [NA, P, 2048], BF16).ap()
        wB = nc.dram_tensor("wB", [NB, P, FF], BF16).ap()
        self.wA = [wA[i] for i in range(NA)]
        self.wB = [wB[i] for i in range(NB)]
        self.wA_res = mkres(NA)
        self.wB_res = mkres(NB)
        self.xs = nc.dram_tensor("xs", [NT, P, DC * T], F32).ap()
        self.xs_res = mkres(NT)
        self.qk_s = nc.dram_tensor("qk_s", [3, 2, NHEAD, P, S], BF16).ap()
        self.v_s = nc.dram_tensor("v_s", [3, S, D], BF16).ap()
        self.ao_s = nc.dram_tensor("ao_s", [NHEAD, P, S], BF16).ap()
        self.att_res = Res()
        self.ao_res = Res()

        with contextlib.ExitStack() as top:
            sb = lambda name, shape, dt: top.enter_context(nc.sbuf_tensor(name, list(shape), dt))
            self.banks = [Bank(top.enter_context(nc.psum_tensor(f"bank{i}", [P, 512], F32))) for i in range(8)]
            self.bank_i = 0
            self.ident_f = sb("ident_f", [P, P], F32)
            self.ident_b = sb("ident_b", [P, P], BF16)
            self.ones_b = sb("ones_b", [P, P], BF16)
            self.mask_b = sb("mask_b", [P, 256], BF16)
            self.vcol = sb("vcol", [P, DC, NVR], F32)
            self.g32 = sb("g32", [P, DC, 7], F32)
            self.qkcol = sb("qkcol", [P, 6], F32)
            self.epsc = sb("epsc", [P, 2], F32)
            self.c_res = Res()
            self.stage_ch = [pg.chan(f"stg{i}") for i in range(2)]
            self.cst_ch = [pg.chan(f"cst{i}") for i in range(2)]
            self.ch_misc = pg.chan("misc")

            with contextlib.ExitStack() as es:
                sb1 = lambda name, shape, dt: es.enter_context(nc.sbuf_tensor(name, list(shape), dt))
                vrow = sb1("vrow_t", [NVR, D], F32)
                qkrow = sb1("qkrow_t", [6, 128], F32)
                maskf = sb1("maskf", [P, 256], F32)
                r_v, r_q, r_m, r_i = Res(), Res(), Res(), Res()
                chs = [pg.chan(f"su{i}") for i in range(4)]
                op("sp", lambda e: e.dma_start(out=vrow[:, :], in_=d_vrow), writes=[r_v], chan=chs[0])
                op("sp", lambda e: e.dma_start(out=qkrow[:, :], in_=d_qkn), writes=[r_q], chan=chs[1])
                op("sp", lambda e: e.dma_start(out=maskf[:, :], in_=d_mask), writes=[r_m], chan=chs[2])
                op("sp", lambda e: e.dma_start(out=self.ident_f[:, :], in_=d_ident), writes=[r_i], chan=chs[3])
                op("pool", lambda e: e.tensor_copy(out=self.ident_b[:, :], in_=self.ident_f[:, :]),
                   reads=[r_i], writes=[self.c_res])
                op("pool", lambda e: e.tensor_copy(out=self.mask_b[:, :], in_=maskf[:, :]),
                   reads=[r_m], writes=[self.c_res])
                op("pool", lambda e: e.memset(self.ones_b[:, :], 1.0), writes=[self.c_res])
                op("pool", lambda e: e.memset(self.epsc[:, 0:1], float(D * EPS)), writes=[self.c_res])
                op("pool", lambda e: e.memset(self.epsc[:, 1:2], float(128 * EPS)), writes=[self.c_res])
                for dc in range(DC):
                    b = self.bank()
                    op("pe", lambda e, b=b, dc=dc: e.transpose(out=b.t[:, 0:NVR], in_=vrow[:, dc * 128:(dc + 1) * 128],
                                                               identity=self.ident_f[0:NVR, 0:NVR]),
                       reads=[r_v, r_i], writes=[b.res])
                    op("dve", lambda e, b=b, dc=dc: e.tensor_copy(out=self.vcol[:, dc, :], in_=b.t[:, 0:NVR]),
                       reads=[b.res], writes=[self.c_res])
                b = self.bank()
                op("pe", lambda e, b=b: e.transpose(out=b.t[:, 0:6], in_=qkrow[:, :], identity=self.ident_f[0:6, 0:6]),
                   reads=[r_q, r_i], writes=[b.res])
                op("dve", lambda e, b=b: e.tensor_copy(out=self.qkcol[:, :], in_=b.t[:, 0:6]),
                   reads=[b.res], writes=[self.c_res])
                op("dve", lambda e: e.tensor_scalar(out=self.g32[:, :, 0:6], in0=self.vcol[:, :, 0:6], scalar1=float(np.sqrt(D)),
                                                    scalar2=None, op0=ALU.mult), reads=[self.c_res], writes=[self.c_res])
                op("dve", lambda e: e.tensor_scalar(out=self.g32[:, :, 6:7], in0=self.vcol[:, :, 9:10], scalar1=float(np.sqrt(D)),
                                                    scalar2=None, op0=ALU.mult), reads=[self.c_res], writes=[self.c_res])
                pg.barrier()
                pg.emit()

            upto = self.upto
            order = ["load", "ffn0a", "conv", "ffn0b", "ffn1a", "qkv", "full"]
            lvl = order.index(upto)
            with contextlib.ExitStack() as es:
                self.sbp = lambda name, shape, dt: es.enter_context(nc.sbuf_tensor("p1_" + name, list(shape), dt))
                self.alloc_common()
                sbp = self.sbp
                self.x_tm = sbp("x_tm", [P, 4, D], F32)
                self.xtm_res = Res()
                self.ch_xin = pg.chan("xin")
                self.ch_xout = pg.chan("xout")
                self.u_ext = sbp("u_ext", [P, DC, CW - 1 + T], BF16)
                self.u_res = Res()
                self.dg = [sbp(f"dg{i}", [P, CW, P], BF16) for i in range(2)]
                self.dg_res = mkres(2)
                self.sqh = [sbp(f"sqh{i}", [P, T], BF16) for i in range(3)]
                self.sqh_res = mkres(3)
                self.r2 = [sbp(f"r2_{i}", [P, T], F32) for i in range(3)]
                self.r2_res = mkres(3)
                self.qst = [sbp(f"qst{i}", [P, T], BF16) for i in range(3)]
                self.qst_res = mkres(3)
                self.qst_ch = [pg.chan(f"qst{i}") for i in range(3)]
                self.qst_i = 0
                self.vst = sbp("vst", [P, 4, D], BF16)
                self.vst_res = Res()
                self.vst_ch = pg.chan("vst")
                planA, planB = [], []
                for i in range(NT):
                    if lvl >= 1:
                        planA += list(range(0, 22)); planB += list(range(0, 8))
                    if lvl >= 2:
                        planA += list(range(88, 96)); planB += list(range(32, 40))
                    if lvl >= 3:
                        planA += list(range(22, 44)); planB += list(range(8, 16))
                    if lvl >= 4:
                        planA += list(range(44, 66)); planB += list(range(16, 24))
                    if lvl >= 5:
                        planA += list(range(96, 132))
                self.wsA = WStream(self, "A", self.slotsA, planA)
                self.wsB = WStream(self, "B", self.slotsB, planB)
                for i in range(NT):
                    self.load_x(i)
                    if lvl >= 1:
                        self.ffn(0, 0, 0)
                    if lvl >= 2:
                        self.conv(i)
                    if lvl >= 3:
                        self.ffn(1, 2, 8)
                    if lvl >= 4:
                        self.ffn(2, 3, 16)
                    if lvl >= 5:
                        self.qkv(i)
                        op(STQ, lambda e, i=i: e.dma_start(out=self.xs[i], in_=self.xT[:, :, :].rearrange("p a t -> p (a t)")),
                           reads=self.xT_res, writes=[self.xs_res[i]], chan=self.ch_xout)
                    else:
                        self.store_out(i, self.x_tm, self.xtm_res)
                self.flush_stores(0)
                pg.barrier()
                pg.emit()
            if lvl < 5:
                return
            with contextlib.ExitStack() as es:
                self.sbp = lambda name, shape, dt: es.enter_context(nc.sbuf_tensor("p2_" + name, list(shape), dt))
                self.attention()
                pg.barrier()
                pg.emit()
            with contextlib.ExitStack() as es:
                self.sbp = lambda name, shape, dt: es.enter_context(nc.sbuf_tensor("p3_" + name, list(shape), dt))
                self.alloc_common()
                sbp = self.sbp
                self.otm = sbp("otm", [P, 4, D], F32)
                self.otm_res = Res()
                self.ch_xin = pg.chan("xin3")
                self.ch_xout = pg.chan("xout3")
                aot = sbp("aot", [P, NHEAD, T], BF16)
                aot_res = Res()
                ch_ao = pg.chan("aoin")
                planA, planB = [], []
                for i in range(NT):
                    planB += list(range(40, 48))
                    planA += list(range(66, 88)); planB += list(range(24, 32))
                self.wsA = WStream(self, "A", self.slotsA, planA)
                self.wsB = WStream(self, "B", self.slotsB, planB)
                for i in range(NT):
                    op("sp", lambda e, i=i: e.dma_start(out=self.xT[:, :, :].rearrange("p a t -> p (a t)"), in_=self.xs[i]),
                       reads=[self.xs_res[i]], writes=self.xT_res, chan=self.ch_xin)
                    op("sp", lambda e, i=i: e.dma_start(out=aot[:, :, :],
                                                        in_=self.ao_s[:, :, i * T:(i + 1) * T].rearrange("h e t -> e h t")),
                       reads=[self.ao_res], writes=[aot_res], chan=ch_ao)
                    pend = None
                    for dc in range(DC):
                        slot, sres = self.wsB.acquire()
                        sv = slot[:, 0:DC * 128].rearrange("p (a c) -> p a c", c=128)
                        b = self.bank()
                        for hh in range(NHEAD):
                            op("pe", lambda e, b=b, sv=sv, hh=hh: e.matmul(b.t[:, :], sv[:, hh, :], aot[:, hh, :],
                                                                           start=(hh == 0), stop=(hh == NHEAD - 1)),
                               reads=[sres, aot_res], writes=[b.res], inc=(hh == NHEAD - 1))
                        if pend is not None:
                            pend()
                            pend = None
                        op("dve", lambda e, b=b, dc=dc: e.tensor_tensor(out=self.xT[:, dc, :], in0=b.t[:, :], in1=self.xT[:, dc, :],
                                                                        op=ALU.add),
                           reads=[b.res, self.xT_res[dc]], writes=[self.xT_res[dc]])
                        pend = self.stat_chunk(self.xT[:, dc, :], [self.xT_res[dc]], dc)
                    pend()
                    self.ffn(3, 5, 24, stats=False)
                    self.store_out(i, self.otm, self.otm_res)
                self.flush_stores(0)
                pg.barrier()
                pg.emit()

    def alloc_common(self):
        sbp = self.sbp
        self.stageF = [sbp(f"stageF{i}", [P, FF], F32) for i in range(2)]
        self.stage_res = mkres(2)
        self.slotsA = [sbp(f"slotA{i}", [P, 2048], BF16) for i in range(4)]
        self.slotsB = [sbp(f"slotB{i}", [P, FF], BF16) for i in range(3)]
        self.xT = sbp("xT", [P, DC, T], F32)
        self.xT_res = mkres(DC)
        self.sq = sbp("sq", [P, DC, T], BF16)
        self.sq_res = mkres(DC)
        self.h = sbp("h", [P, DC, T], BF16)
        self.h_res = mkres(DC)
        self.rstd = sbp("rstd", [P, T], F32)
        self.rstd_res = Res()
        self.Gflat = sbp("G", [P, FC * T], BF16)
        self.G = self.Gflat[:, :].rearrange("p (j t) -> p j t", t=T)
        self.cvf = self.Gflat[:, 0:16 * T].bitcast(F32).rearrange("p (c t) -> p c t", t=T)
        self.G_res = mkres(FC)
        self.sg = [sbp(f"sg{i}", [P, T], F32) for i in range(2)]
        self.sg_res = mkres(2)
        self.sg_i = 0

    def load_x(self, i):
        op = self.pg.op
        op("sp", lambda e: e.dma_start(out=self.x_tm[:, :, :],
                                       in_=self.d_x[i * T:(i + 1) * T, :].rearrange("(b p) d -> p b d", p=P)),
           writes=[self.xtm_res], chan=self.ch_xin)
        pend = None
        for dc in range(DC):
            b = self.bank()
            for tb in range(4):
                op("pe", lambda e, b=b, dc=dc, tb=tb: e.transpose(out=b.t[:, tb * 128:(tb + 1) * 128],
                                                                  in_=self.x_tm[:, tb, dc * 128:(dc + 1) * 128],
                                                                  identity=self.ident_f[:, :]),
                   reads=[self.xtm_res], writes=[b.res], inc=(tb == 3))
            if pend is not None:
                pend()
                pend = None
            if dc % 2 == 0:
                op("dve", lambda e, b=b, dc=dc: e.tensor_copy(out=self.xT[:, dc, :], in_=b.t[:, :]),
                   reads=[b.res], writes=[self.xT_res[dc]])
            else:
                op("act", lambda e, b=b, dc=dc: e.activation(out=self.xT[:, dc, :], in_=b.t[:, :], func=AF.Identity),
                   reads=[b.res], writes=[self.xT_res[dc]])
            pend = self.stat_chunk(self.xT[:, dc, :], [self.xT_res[dc]], dc)
        if pend is not None:
            pend()

    def store_out(self, i, otm, otm_res):
        op = self.pg.op
        for tb in range(4):
            for half in range(2):
                b = self.bank()
                for k in range(4):
                    dc = half * 4 + k
                    op("pe", lambda e, b=b, dc=dc, tb=tb, k=k: e.transpose(out=b.t[:, k * 128:(k + 1) * 128],
                                                                           in_=self.xT[:, dc, tb * 128:(tb + 1) * 128],
                                                                           identity=self.ident_f[:, :]),
                       reads=[self.xT_res[dc]], writes=[b.res], inc=(k == 3))
                if half == 0:
                    op("dve", lambda e, b=b, tb=tb: e.tensor_copy(out=otm[:, tb, 0:512], in_=b.t[:, :]),
                       reads=[b.res], writes=[otm_res])
                else:
                    op("act", lambda e, b=b, tb=tb: e.activation(out=otm[:, tb, 512:1024], in_=b.t[:, :], func=AF.Identity),
                       reads=[b.res], writes=[otm_res])
        op(STQ, lambda e: e.dma_start(out=self.d_out[i * T:(i + 1) * T, :].rearrange("(b p) d -> p b d", p=P),
                                        in_=otm[:, :, :]),
           reads=[otm_res], writes=[], chan=self.ch_xout)

    def stat_chunk(self, src, src_res, dc):
        op = self.pg.op
        op("act", lambda e: e.activation(out=self.sq[:, dc, :], in_=src, func=AF.Square),
           reads=src_res, writes=[self.sq_res[dc]])
        b = self.banks[7]

        def pe():
            op("pe", lambda e: e.matmul(b.t[:, :], self.ones_b[:, :], self.sq[:, dc, :], start=(dc == 0), stop=(dc == DC - 1)),
               reads=[self.sq_res[dc], self.c_res], writes=[b.res], inc=(dc == DC - 1))
        return pe

    def rsqrt(self, b, dst, dst_res, epsname):
        op = self.pg.op
        col = 0 if epsname == "epsD" else 1
        op("act", lambda e: e.activation(out=dst[:, :], in_=b.t[:, :], func=AF.Ln, bias=self.epsc[:, col:col + 1]),
           reads=[b.res, self.c_res], writes=[dst_res])
        op("act", lambda e: e.activation(out=dst[:, :], in_=dst[:, :], func=AF.Exp, scale=-0.5),
           reads=[dst_res], writes=[dst_res])

    def norm_h(self, gi):
        op = self.pg.op
        self.rsqrt(self.banks[7], self.rstd, self.rstd_res, "epsD")
        for dc in range(DC):
            op("dve", lambda e, dc=dc: e.scalar_tensor_tensor(out=self.h[:, dc, :], in0=self.xT[:, dc, :],
                                                              scalar=self.g32[:, dc, gi:gi + 1], in1=self.rstd[:, :],
                                                              op0=ALU.mult, op1=ALU.mult),
               reads=[self.xT_res[dc], self.rstd_res, self.c_res], writes=[self.h_res[dc]])

    def ffn(self, li, gi, b0, stats=True):
        op = self.pg.op
        self.norm_h(gi)
        for j in range(FC):
            slot, sres = self.wsA.acquire()
            sv = slot[:, :].rearrange("p (a c) -> p a c", c=256)
            bg, bu = self.bank(), self.bank()
            for half, b in ((0, bg), (1, bu)):
                for dc in range(DC):
                    op("pe", lambda e, b=b, sv=sv, dc=dc, half=half: e.matmul(b.t[:, :], sv[:, dc, half * 128:(half + 1) * 128],
                                                                              self.h[:, dc, :], start=(dc == 0), stop=(dc == DC - 1)),
                       reads=[sres, self.h_res[dc]], writes=[b.res], inc=(dc == DC - 1))
            k = self.sg_i % 2
            self.sg_i += 1
            op("act", lambda e, k=k, bg=bg: e.activation(out=self.sg[k][:, :], in_=bg.t[:, :], func=AF.Silu),
               reads=[bg.res], writes=[self.sg_res[k]])
            op("dve", lambda e, k=k, bu=bu, j=j: e.tensor_tensor(out=self.G[:, j, :], in0=bu.t[:, :], in1=self.sg[k][:, :], op=ALU.mult),
               reads=[bu.res, self.sg_res[k]], writes=[self.G_res[j]])
        pend = None
        for dc in range(DC):
            slot, sres = self.wsB.acquire()
            sv = slot[:, :].rearrange("p (a c) -> p a c", c=128)
            b = self.bank()
            for j in range(FC):
                op("pe", lambda e, b=b, sv=sv, j=j: e.matmul(b.t[:, :], sv[:, j, :], self.G[:, j, :], start=(j == 0), stop=(j == FC - 1)),
                   reads=[sres, self.G_res[j]], writes=[b.res], inc=(j == FC - 1))
            if pend is not None:
                pend()
                pend = None
            op("dve", lambda e, b=b, dc=dc: e.scalar_tensor_tensor(out=self.xT[:, dc, :], in0=b.t[:, :], scalar=0.5,
                                                                   in1=self.xT[:, dc, :], op0=ALU.mult, op1=ALU.add),
               reads=[b.res, self.xT_res[dc]], writes=[self.xT_res[dc]])
            if stats:
                pend = self.stat_chunk(self.xT[:, dc, :], [self.xT_res[dc]], dc)
        if pend is not None:
            pend()

    def conv(self, i):
        op = self.pg.op
        HL = CW - 1
        self.norm_h(1)
        if i == 0:
            op("pool", lambda e: e.memset(self.u_ext[:, :, 0:HL], 0.0), writes=[self.u_res])
        else:
            op("pool", lambda e: e.tensor_copy(out=self.u_ext[:, :, 0:HL], in_=self.u_ext[:, :, T:T + HL]),
               reads=[self.u_res], writes=[self.u_res])
        for c in range(DC):
            slot, sres = self.wsA.acquire()
            sv = slot[:, :].rearrange("p (a c) -> p a c", c=256)
            ba, bgt = self.bank(), self.bank()
            for half, b in ((0, ba), (1, bgt)):
                for dc in range(DC):
                    op("pe", lambda e, b=b, sv=sv, dc=dc, half=half: e.matmul(b.t[:, :], sv[:, dc, half * 128:(half + 1) * 128],
                                                                              self.h[:, dc, :], start=(dc == 0), stop=(dc == DC - 1)),
                       reads=[sres, self.h_res[dc]], writes=[b.res], inc=(dc == DC - 1))
            k = self.sg_i % 2
            self.sg_i += 1
            op("act", lambda e, k=k, b=bgt, c=c: e.activation(out=self.sg[k][:, :], in_=b.t[:, :], func=AF.Sigmoid,
                                                              bias=self.vcol[:, c, 7:8]),
               reads=[bgt.res, self.c_res], writes=[self.sg_res[k]])
            op("dve", lambda e, k=k, b=ba, c=c: e.scalar_tensor_tensor(out=self.u_ext[:, c, HL:HL + T], in0=b.t[:, :],
                                                                       scalar=self.vcol[:, c, 6:7], in1=self.sg[k][:, :],
                                                                       op0=ALU.add, op1=ALU.mult),
               reads=[ba.res, self.sg_res[k], self.c_res], writes=[self.u_res])
        cvf = self.cvf
        cres = lambda c: [self.G_res[2 * c], self.G_res[2 * c + 1]]
        pend = None
        for c in range(DC):
            k = c % 2
            op("pool", lambda e, k=k, c=c: e.tensor_tensor(out=self.dg[k][:, :, :],
                                                           in0=self.ident_b[:, :].unsqueeze(1).to_broadcast([P, CW, P]),
                                                           in1=self.vcol[:, c, 11:11 + CW].unsqueeze(2).to_broadcast([P, CW, P]),
                                                           op=ALU.mult),
               reads=[self.c_res], writes=[self.dg_res[k]])
            b = self.bank()
            for j in range(CW):
                op("pe", lambda e, b=b, k=k, c=c, j=j: e.matmul(b.t[:, :], self.dg[k][:, j, :], self.u_ext[:, c, j:j + T],
                                                                start=(j == 0), stop=(j == CW - 1)),
                   reads=[self.dg_res[k], self.u_res], writes=[b.res], inc=(j == CW - 1))
            if pend is not None:
                pend()
                pend = None
            op("act", lambda e, b=b, c=c: e.activation(out=cvf[:, c, :], in_=b.t[:, :], func=AF.Identity, bias=self.vcol[:, c, 8:9]),
               reads=[b.res, self.c_res], writes=cres(c))
            pend = self.stat_chunk(cvf[:, c, :], cres(c), c)
        if pend is not None:
            pend()
            pend = None
        self.rsqrt(self.banks[7], self.rstd, self.rstd_res, "epsD")
        for c in range(DC):
            op("dve", lambda e, c=c: e.scalar_tensor_tensor(out=cvf[:, c, :], in0=cvf[:, c, :], scalar=self.g32[:, c, 6:7],
                                                            in1=self.rstd[:, :], op0=ALU.mult, op1=ALU.mult),
               reads=cres(c) + [self.rstd_res, self.c_res], writes=cres(c))
            op("act", lambda e, c=c: e.activation(out=self.h[:, c, :], in_=cvf[:, c, :], func=AF.Silu),
               reads=cres(c), writes=[self.h_res[c]])
        for dc in range(DC):
            slot, sres = self.wsB.acquire()
            sv = slot[:, 0:DC * 128].rearrange("p (a c) -> p a c", c=128)
            b = self.bank()
            for c in range(DC):
                op("pe", lambda e, b=b, sv=sv, c=c: e.matmul(b.t[:, :], sv[:, c, :], self.h[:, c, :], start=(c == 0), stop=(c == DC - 1)),
                   reads=[sres, self.h_res[c]], writes=[b.res], inc=(c == DC - 1))
            if pend is not None:
                pend()
                pend = None
            op("dve", lambda e, b=b, dc=dc: e.scalar_tensor_tensor(out=self.xT[:, dc, :], in0=b.t[:, :], scalar=self.vcol[:, dc, 10:11],
                                                                   in1=self.xT[:, dc, :], op0=ALU.add, op1=ALU.add),
               reads=[b.res, self.xT_res[dc], self.c_res], writes=[self.xT_res[dc]])
            pend = self.stat_chunk(self.xT[:, dc, :], [self.xT_res[dc]], dc)
        if pend is not None:
            pend()

    def qkv(self, i):
        op = self.pg.op
        self.norm_h(4)
        pend = None
        for g in range(3):
            for hh in range(NHEAD):
                slot, sres = self.wsA.acquire()
                sv = slot[:, :].rearrange("p (a c) -> p a c", c=256)
                for which in range(2):
                    b1 = self.bank()
                    for dc in range(DC):
                        op("pe", lambda e, b=b1, sv=sv, dc=dc, which=which: e.matmul(b.t[:, :], sv[:, dc, which * 128:(which + 1) * 128],
                                                                                     self.h[:, dc, :], start=(dc == 0), stop=(dc == DC - 1)),
                           reads=[sres, self.h_res[dc]], writes=[b1.res], inc=(dc == DC - 1))
                    k = self.qk_i % 3
                    self.qk_i += 1
                    op("act", lambda e, k=k, b=b1: e.activation(out=self.sqh[k][:, :], in_=b.t[:, :], func=AF.Square),
                       reads=[b1.res], writes=[self.sqh_res[k]])
                    if pend is not None:
                        pend()
                        pend = None

                    def tail(k=k, b1=b1, g=g, which=which, hh=hh):
                        b2 = self.bank()
                        op("pe", lambda e: e.matmul(b2.t[:, :], self.ones_b[:, :], self.sqh[k][:, :], start=True, stop=True),
                           reads=[self.sqh_res[k], self.c_res], writes=[b2.res])
                        r2, r2r = self.r2[k], self.r2_res[k]
                        op("act", lambda e: e.activation(out=r2[:, :], in_=b2.t[:, :], func=AF.Ln, bias=self.epsc[:, 1:2]),
                           reads=[b2.res, self.c_res], writes=[r2r])

                        def stage_b():
                            op("act", lambda e: e.activation(out=r2[:, :], in_=r2[:, :], func=AF.Exp, scale=-0.5),
                               reads=[r2r], writes=[r2r])
                            qi = self.qst_i % 3
                            self.qst_i += 1
                            col = which * 3 + g
                            op("dve", lambda e: e.scalar_tensor_tensor(
                                out=self.qst[qi][:, :], in0=b1.t[:, :], scalar=self.qkcol[:, col:col + 1], in1=r2[:, :],
                                op0=ALU.mult, op1=ALU.mult),
                               reads=[b1.res, r2r, self.c_res], writes=[self.qst_res[qi]])
                            while self.qstores:
                                self.qstores.pop(0)()
                            self.qstores.append(lambda: op(
                                STQ, lambda e: e.dma_start(out=self.qk_s[g, which, hh, :, i * T:(i + 1) * T], in_=self.qst[qi][:, :]),
                                reads=[self.qst_res[qi]], writes=[self.att_res], chan=self.qst_ch[qi]))
                        prevb = self.pend_b
                        self.pend_b = stage_b
                        if prevb is not None:
                            prevb()
                    pend = tail
        if pend is not None:
            pend()
        if self.pend_b is not None:
            self.pend_b()
            self.pend_b = None
        while self.qstores:
            self.qstores.pop(0)()
        cnt = 0
        for g in range(3):
            for pr in range(4):
                slot, sres = self.wsA.acquire()
                sv = slot[:, :].rearrange("p (a c) -> p a c", c=256)
                for t2 in range(2):
                    b = self.bank()
                    for q in range(2):
                        tb = t2 * 2 + q
                        for dc in range(DC):
                            op("pe", lambda e, b=b, sv=sv, dc=dc, tb=tb, q=q: e.matmul(
                                b.t[:, q * 256:(q + 1) * 256], self.h[:, dc, tb * 128:(tb + 1) * 128], sv[:, dc, :],
                                start=(dc == 0), stop=(dc == DC - 1)),
                               reads=[sres, self.h_res[dc]], writes=[b.res], inc=(dc == DC - 1 and q == 1))
                    outv = self.vst[:, t2 * 2:t2 * 2 + 2, pr * 256:(pr + 1) * 256]
                    inv = b.t[:, :].rearrange("p (q c) -> p q c", c=256)
                    if cnt % 2 == 0:
                        op("dve", lambda e, outv=outv, inv=inv: e.tensor_copy(out=outv, in_=inv), reads=[b.res], writes=[self.vst_res])
                    else:
                        op("act", lambda e, outv=outv, inv=inv: e.activation(out=outv, in_=inv, func=AF.Identity),
                           reads=[b.res], writes=[self.vst_res])
                    cnt += 1
            op(STQ, lambda e, g=g: e.dma_start(out=self.v_s[g, i * T:(i + 1) * T, :].rearrange("(b p) c -> p b c", p=P),
                                                 in_=self.vst[:, :, :]),
               reads=[self.vst_res], writes=[self.att_res], chan=self.vst_ch)

    def attention(self):
        nc, pg, S = self.nc, self.pg, self.S
        op = pg.op
        sbp = self.sbp
        NBLK = S // 128
        qraws = [sbp(f"qraw{i}", [P, S], BF16) for i in range(2)]
        kraws = [sbp(f"kraw{i}", [P, S], BF16) for i in range(2)]
        qraw_ress, kraw_ress = mkres(2), mkres(2)
        ch_qrs = [pg.chan(f"qraw{i}") for i in range(2)]
        ch_krs = [pg.chan(f"kraw{i}") for i in range(2)]
        raw_i = [0]
        NBUF = 3
        qs = [sbp(f"qs{i}", [P, S], BF16) for i in range(NBUF)]
        ks = [sbp(f"ks{i}", [P, S], BF16) for i in range(NBUF)]
        vs = [sbp(f"vs{i}", [P, NBLK, P], BF16) for i in range(NBUF)]
        qs_res, ks_res, vs_res = mkres(NBUF), mkres(NBUF), mkres(NBUF)
        ch_q = [pg.chan(f"qs{i}") for i in range(NBUF)]
        ch_k = [pg.chan(f"ks{i}") for i in range(NBUF)]
        ch_v = [pg.chan(f"vs{i}") for i in range(NBUF)]
        Pt = sbp("Pt", [P, NBLK, 256], BF16)
        Pt_res = mkres(NBLK)
        oaccs = [sbp(f"oacc{i}", [P, S], F32) for i in range(2)]
        laccs = [sbp(f"lacc{i}", [P, S], F32) for i in range(2)]
        oacc_ress, lacc_ress = mkres(2), mkres(2)
        pend_ao = None
        aost = sbp("aost", [P, S], BF16)
        aost_res = Res()
        ch_ao = pg.chan("aost")
        SC = float(np.sqrt(128.0))
        iters = [(hh, g) for hh in range(NHEAD) for g in range(3)]

        def loads(it):
            hh, g = iters[it]
            d = DIL[g]
            nb = (S // d) // 128
            bi = it % NBUF
            if d == 1:
                op("sp", lambda e, bi=bi, g=g, hh=hh: e.dma_start(out=qs[bi][:, :], in_=self.qk_s[g, 0, hh]),
                   reads=[self.att_res], writes=[qs_res[bi]], chan=ch_q[bi])
                op("sp", lambda e, bi=bi, g=g, hh=hh: e.dma_start(out=ks[bi][:, :], in_=self.qk_s[g, 1, hh]),
                   reads=[self.att_res], writes=[ks_res[bi]], chan=ch_k[bi])
            else:
                ri = raw_i[0] % 2
                raw_i[0] += 1
                qraw, kraw = qraws[ri], kraws[ri]
                qraw_res, kraw_res = qraw_ress[ri], kraw_ress[ri]
                ch_qr, ch_kr = ch_qrs[ri], ch_krs[ri]
                op("sp", lambda e, g=g, hh=hh: e.dma_start(out=qraw[:, :], in_=self.qk_s[g, 0, hh]),
                   reads=[self.att_res], writes=[qraw_res], chan=ch_qr)
                op("sp", lambda e, g=g, hh=hh: e.dma_start(out=kraw[:, :], in_=self.qk_s[g, 1, hh]),
                   reads=[self.att_res], writes=[kraw_res], chan=ch_kr)
                op("pool", lambda e, bi=bi, d=d: e.tensor_copy(out=qs[bi][:, :].rearrange("p (r m) -> p r m", r=d),
                                                               in_=qraw[:, :].rearrange("p (m r) -> p r m", r=d)),
                   reads=[qraw_res], writes=[qs_res[bi]])
                op("pool", lambda e, bi=bi, d=d: e.tensor_copy(out=ks[bi][:, :].rearrange("p (r m) -> p r m", r=d),
                                                               in_=kraw[:, :].rearrange("p (m r) -> p r m", r=d)),
                   reads=[kraw_res], writes=[ks_res[bi]])
            for r in range(d):
                src = self.v_s[g, :, hh * 128:(hh + 1) * 128].rearrange("(kb p r) e -> r p kb e", p=P, r=d)[r]
                op("sp", lambda e, bi=bi, src=src, r=r, nb=nb: e.dma_start(out=vs[bi][:, r * nb:(r + 1) * nb, :], in_=src),
                   reads=[self.att_res], writes=[vs_res[bi]], chan=ch_v[bi], chain=(r > 0))

        loads(0)
        loads(1)
        it = 0
        for hh in range(NHEAD):
            oacc, lacc = oaccs[hh % 2], laccs[hh % 2]
            oacc_res, lacc_res = oacc_ress[hh % 2], lacc_ress[hh % 2]
            for g in range(3):
                d = DIL[g]
                L = S // d
                nb = L // 128
                bi = it % NBUF
                if it + 2 < len(iters):
                    loads(it + 2)
                it += 1
                def scores(r, n0):
                    base = r * L
                    for kb in range(n0, min(n0 + 4, nb)):
                        nq = 2 if kb < nb - 1 else 1
                        blk = r * nb + kb
                        b = self.bank()
                        op("pe", lambda e, b=b, bi=bi, base=base, kb=kb, nq=nq: e.matmul(
                            b.t[:, 0:128 * nq], ks[bi][:, base + kb * 128:base + (kb + 1) * 128],
                            qs[bi][:, base + kb * 128:base + (kb + nq) * 128], start=True, stop=False),
                           reads=[ks_res[bi], qs_res[bi]], writes=[b.res], inc=False)
                        op("pe", lambda e, b=b, nq=nq: e.matmul(b.t[:, 0:128 * nq], self.ident_b[:, :], self.mask_b[:, 0:128 * nq],
                                                                start=False, stop=True),
                           reads=[self.c_res], writes=[b.res])
                        op("act", lambda e, b=b, blk=blk, nq=nq: e.activation(out=Pt[:, blk, 0:128 * nq], in_=b.t[:, 0:128 * nq],
                                                                             func=AF.Exp, scale=SC),
                           reads=[b.res], writes=[Pt_res[blk]])
                def pv(r, n0):
                    nn = min(4, nb - n0)
                    bo, bl = self.bank(), self.bank()
                    blk0 = r * nb + n0
                    for n in range(n0, n0 + nn):
                        blk = r * nb + n
                        o = bo.t[:, (n - n0) * 128:(n - n0 + 1) * 128]
                        last = (n == n0 + nn - 1)
                        if n > 0:
                            op("pe", lambda e, o=o, bi=bi, blk=blk: e.matmul(o, vs[bi][:, blk - 1, :], Pt[:, blk - 1, 128:256],
                                                                             start=True, stop=False),
                               reads=[vs_res[bi], Pt_res[blk - 1]], writes=[bo.res], inc=False)
                        op("pe", lambda e, o=o, bi=bi, blk=blk, n=n: e.matmul(o, vs[bi][:, blk, :], Pt[:, blk, 0:128],
                                                                              start=(n == 0), stop=True),
                           reads=[vs_res[bi], Pt_res[blk]], writes=[bo.res], inc=last)
                    if n0 == 0:
                        op("pe", lambda e, bl=bl, blk0=blk0: e.matmul(bl.t[:, 0:128], self.ones_b[:, :], Pt[:, blk0, 0:128],
                                                                      start=True, stop=True),
                           reads=[Pt_res[blk0], self.c_res], writes=[bl.res], inc=(nn == 1))
                        if nn > 1:
                            op("pe", lambda e, bl=bl, blk0=blk0, nn=nn: e.matmul(bl.t[:, 128:nn * 128], self.ones_b[:, :],
                                                                                 Pt[:, blk0 + 1:blk0 + nn, 0:128], start=True, stop=False),
                               reads=[Pt_res[blk0 + q] for q in range(1, nn)], writes=[bl.res], inc=False)
                            op("pe", lambda e, bl=bl, blk0=blk0, nn=nn: e.matmul(bl.t[:, 128:nn * 128], self.ones_b[:, :],
                                                                                 Pt[:, blk0:blk0 + nn - 1, 128:256], start=False, stop=True),
                               reads=[Pt_res[blk0 + q] for q in range(0, nn - 1)], writes=[bl.res])
                    else:
                        op("pe", lambda e, bl=bl, blk0=blk0, nn=nn: e.matmul(bl.t[:, 0:nn * 128], self.ones_b[:, :],
                                                                             Pt[:, blk0:blk0 + nn, 0:128], start=True, stop=False),
                           reads=[Pt_res[blk0 + q] for q in range(nn)], writes=[bl.res], inc=False)
                        op("pe", lambda e, bl=bl, blk0=blk0, nn=nn: e.matmul(bl.t[:, 0:nn * 128], self.ones_b[:, :],
                                                                             Pt[:, blk0 - 1:blk0 + nn - 1, 128:256], start=False, stop=True),
                           reads=[Pt_res[blk0 - 1 + q] for q in range(nn)], writes=[bl.res])
                    lo = n0 * 128 * d + r
                    cntq = nn * 128
                    ov = oacc[:, :].rearrange("p (m r) -> p r m", r=d)[:, r, n0 * 128:n0 * 128 + cntq]
                    lv = lacc[:, :].rearrange("p (m r) -> p r m", r=d)[:, r, n0 * 128:n0 * 128 + cntq]
                    if g == 0:
                        op("dve", lambda e, ov=ov, bo=bo, cntq=cntq: e.tensor_copy(out=ov, in_=bo.t[:, 0:cntq]),
                           reads=[bo.res], writes=[oacc_res])
                        op("act", lambda e, lv=lv, bl=bl, cntq=cntq: e.activation(out=lv, in_=bl.t[:, 0:cntq], func=AF.Identity),
                           reads=[bl.res], writes=[lacc_res])
                    else:
                        op("dve", lambda e, ov=ov, bo=bo, cntq=cntq: e.tensor_tensor(out=ov, in0=bo.t[:, 0:cntq], in1=ov, op=ALU.add),
                           reads=[bo.res, oacc_res], writes=[oacc_res])
                        op("dve", lambda e, lv=lv, bl=bl, cntq=cntq: e.tensor_tensor(out=lv, in0=bl.t[:, 0:cntq], in1=lv, op=ALU.add),
                           reads=[bl.res, lacc_res], writes=[lacc_res])

                units = [(r, n0) for r in range(d) for n0 in range(0, nb, 4)]
                scores(*units[0])
                for ui, u in enumerate(units):
                    if ui + 1 < len(units):
                        scores(*units[ui + 1])
                    pv(*u)
                if g == 0 and pend_ao is not None:
                    pend_ao()
                    pend_ao = None
            def fin(hh=hh, oacc=oacc, lacc=lacc, oacc_res=oacc_res, lacc_res=lacc_res):
                op("act", lambda e: e.activation(out=lacc[:, :], in_=lacc[:, :], func=AF.Ln), reads=[lacc_res], writes=[lacc_res])
                op("act", lambda e: e.activation(out=lacc[:, :], in_=lacc[:, :], func=AF.Exp, scale=-1.0),
                   reads=[lacc_res], writes=[lacc_res])
                op("dve", lambda e: e.tensor_tensor(out=aost[:, :], in0=oacc[:, :], in1=lacc[:, :], op=ALU.mult),
                   reads=[oacc_res, lacc_res], writes=[aost_res])
                op(STQ, lambda e: e.dma_start(out=self.ao_s[hh], in_=aost[:, :]),
                   reads=[aost_res], writes=[self.ao_res], chan=ch_ao)
            pend_ao = fin
        pend_ao()


_CACHE = {}


def host_consts():
    ident = np.eye(P, dtype=np.float32)
    j = np.arange(P)[:, None]
    i = np.arange(P)[None, :]
    cur = np.where(i >= j, 0.0, NEG)
    prev = np.where(j >= i, 0.0, NEG)
    mask = np.concatenate([cur, prev], axis=1).astype(np.float32)
    return ident, mask


def make_in_maps(inputs, n_cores=8, S=4096):
    f = lambda a: np.ascontiguousarray(np.asarray(a, dtype=np.float32))
    ident, mask = host_consts()
    vrow = np.concatenate([
        f(inputs["norm_g"]).reshape(6, D),
        f(inputs["conv_b_pw1"]).reshape(2, D),
        f(inputs["conv_b_dw"]).reshape(1, D),
        f(inputs["conv_norm_g"]).reshape(1, D),
        f(inputs["conv_b_pw2"]).reshape(1, D),
        f(inputs["conv_w_dw"]).reshape(CW, D),
    ], axis=0)
    qkn = np.concatenate([f(inputs["attn_q_norm"]).reshape(3, 128), f(inputs["attn_k_norm"]).reshape(3, 128)], axis=0)
    shared = {
        "vrow": f(vrow), "qkn": f(qkn), "ident": ident, "maskneg": mask,
        "ffn_w_in": f(inputs["ffn_w_in"]).reshape(4, D, 2 * FF),
        "ffn_w_out": f(inputs["ffn_w_out"]).reshape(4, FF, D),
        "conv_w_pw1": f(inputs["conv_w_pw1"]).reshape(D, 2 * D),
        "conv_w_pw2": f(inputs["conv_w_pw2"]).reshape(D, D),
        "attn_w_qkv": f(inputs["attn_w_qkv"]).reshape(D, 9216),
        "attn_w_o": f(inputs["attn_w_o"]).reshape(D, D),
    }
    x = f(inputs["x"])
    maps = []
    for c in range(n_cores):
        m = dict(shared)
        m["x"] = np.ascontiguousarray(x[c, :S])
        maps.append(m)
    return maps


def kernel(**inputs):
    if "k" not in _CACHE:
        _CACHE["k"] = Kern(S=4096, upto="full")
    k = _CACHE["k"]
    maps = make_in_maps(inputs, 8, 4096)
    res = run_bass_kernel_spmd(k.nc, maps, core_ids=list(range(8)))
    out = np.stack([np.asarray(r["out"], dtype=np.float32) for r in res.results], axis=0)
    return out
```

```python
import contextlib
import numpy as np
import concourse.bass as bass
import concourse.mybir as mybir
from concourse.bass_utils import run_bass_kernel_spmd

F32 = mybir.dt.float32
BF16 = mybir.dt.bfloat16
AF = mybir.ActivationFunctionType
ALU = mybir.AluOpType

P = 128
D = 1024
DC = 8
FF = 2816
FC = 22
T = 512
NHEAD = 8
DIL = (1, 4, 16)
EPS = 1e-6
CW = 31
NEG = -30000.0
NVR = 42
NA = 132
NB = 48
SAME_ENGINE_SYNC = True
STQ = "act"


class Res:
    __slots__ = ("w", "r")

    def __init__(self):
        self.w = None
        self.r = {}


def mkres(n):
    return [Res() for _ in range(n)]


class Chan:
    def __init__(self, nc, name):
        self.sem = nc.alloc_semaphore(name=name)
        self.n = 0


class Prog:
    ENG = ("pe", "act", "dve", "pool", "sp")

    def __init__(self, nc):
        self.nc = nc
        self.sem = {e: nc.alloc_semaphore(name="cs_" + e) for e in self.ENG if e != "sp"}
        self.cnt = {e: 0 for e in self.ENG}
        self.q = {e: [] for e in self.ENG}
        self.seen = {e: {} for e in self.ENG}
        self.chans = []
        self.nops = 0

    def chan(self, name):
        c = Chan(self.nc, f"{name}_{len(self.chans)}")
        self.chans.append(c)
        return c

    def op(self, eng, fn, reads=(), writes=(), inc=True, chan=None, chain=False):
        evs = {}

        def need(ev):
            if ev is None:
                return
            s, v = ev
            k = s.num
            if k not in evs or evs[k][1] < v:
                evs[k] = (s, v)

        for r in reads:
            need(r.w)
        for w in writes:
            need(w.w)
            for ev in w.r.values():
                need(ev)
        if chan is not None and chan.n > 0 and not chain:
            need((chan.sem, chan.n * 16))
        if chain:
            evs = {}
        waits = []
        seen = self.seen[eng]
        for k, (s, v) in evs.items():
            if eng != "sp" and s is self.sem[eng]:
                if eng == "pe" or not SAME_ENGINE_SYNC:
                    continue
            if seen.get(k, 0) >= v:
                continue
            seen[k] = v
            waits.append((s, v))
        if chan is not None:
            chan.n += 1
            ev = (chan.sem, chan.n * 16)
            incs = (chan.sem, 16)
        elif inc:
            self.cnt[eng] += 1
            ev = (self.sem[eng], self.cnt[eng])
            incs = (self.sem[eng], 1)
        else:
            assert eng == "pe"
            ev = (self.sem[eng], self.cnt[eng] + 1)
            incs = None
        self.q[eng].append((waits, fn, incs))
        self.nops += 1
        k = ev[0].num
        for r in reads:
            if k not in r.r or r.r[k][1] < ev[1]:
                r.r[k] = ev
        for w in writes:
            w.w = ev
            w.r = {}
        return ev

    def barrier(self):
        evs = []
        for e in self.ENG:
            if e != "sp" and self.cnt[e] > 0:
                evs.append((self.sem[e], self.cnt[e]))
        for c in self.chans:
            if c.n > 0:
                evs.append((c.sem, c.n * 16))
        for e in self.ENG:
            waits = []
            for s, v in evs:
                if e != "sp" and s is self.sem[e]:
                    continue
                if self.seen[e].get(s.num, 0) >= v:
                    continue
                self.seen[e][s.num] = v
                waits.append((s, v))
            self.q[e].append((waits, None, None))

    def emit(self):
        nc = self.nc

        def mk(e):
            items = self.q[e]

            def body(engine):
                for waits, fn, incs in items:
                    for s, v in waits:
                        engine.wait_ge(s, v)
                    if fn is not None:
                        ins = fn(engine)
                        if incs is not None:
                            ins.then_inc(incs[0], incs[1])
            return body

        with nc.Block() as block:
            block.tensor(mk("pe"))
            block.scalar(mk("act"))
            block.vector(mk("dve"))
            block.gpsimd(mk("pool"))
            block.sync(mk("sp"))
        self.q = {e: [] for e in self.ENG}


class Bank:
    def __init__(self, t):
        self.t = t
        self.res = Res()


class WStream:
    def __init__(self, K, kind, slots, plan):
        self.K = K
        self.kind = kind
        self.slots = slots
        self.sres = mkres(len(slots))
        self.chs = [K.pg.chan(f"w{kind}{i}") for i in range(len(slots))]
        self.plan = plan
        self.pos = 0
        self.filled = 0

    def acquire(self):
        n = len(self.slots)
        lim = min(len(self.plan), self.pos + n)
        while self.filled < lim:
            self._fill(self.filled)
            self.filled += 1
        k = self.pos
        self.pos += 1
        return self.slots[k % n], self.sres[k % n]

    def _fill(self, k):
        K = self.K
        pg = K.pg
        n = len(self.slots)
        s = k % n
        idx = self.plan[k]
        slot, sres, ch = self.slots[s], self.sres[s], self.chs[s]
        conv = K.convA if self.kind == "A" else K.convB
        scr = K.wA if self.kind == "A" else K.wB
        scr_res = K.wA_res if self.kind == "A" else K.wB_res
        nel = 2048 if self.kind == "A" else K.srcB(idx)[1] * 128
        if idx in conv:
            K.flush_stores(0)
            pg.op("sp", lambda e, o=slot[:, 0:nel], i=scr[idx][:, 0:nel]: e.dma_start(out=o, in_=i),
                  reads=[scr_res[idx]], writes=[sres], chan=ch)
            return
        conv.add(idx)
        si = K.stage_i
        K.stage_i += 1
        st, stres, stch = K.stageF[si % 3], K.stage_res[si % 3], K.stage_ch[si % 3]
        if self.kind == "A":
            src, ca, cb = K.srcA(idx)
            stv = st[:, 0:2048].rearrange("p (a c) -> p a c", c=256)
            sv = src.rearrange("(a p) c -> p a c", p=P)
            pg.op("sp", lambda e, o=stv[:, :, 0:128], i=sv[:, :, ca:ca + 128]: e.dma_start(out=o, in_=i),
                  writes=[stres], chan=stch)
            pg.op("sp", lambda e, o=stv[:, :, 128:256], i=sv[:, :, cb:cb + 128]: e.dma_start(out=o, in_=i),
                  writes=[stres], chan=stch, chain=True)
            nel = 2048
        else:
            src, kc, c0 = K.srcB(idx)
            nel = kc * 128
            stv = st[:, 0:nel].rearrange("p (a c) -> p a c", c=128)
            sv = src.rearrange("(a p) c -> p a c", p=P)
            pg.op("sp", lambda e, o=stv, i=sv[:, :, c0:c0 + 128]: e.dma_start(out=o, in_=i),
                  writes=[stres], chan=stch)
        ce = ("pool", "dve", "act", "dve", "act")[si % 5]
        if ce == "act":
            pg.op("act", lambda e, o=slot[:, 0:nel], i=st[:, 0:nel]: e.activation(out=o, in_=i, func=AF.Identity),
                  reads=[stres], writes=[sres])
        else:
            pg.op(ce, lambda e, o=slot[:, 0:nel], i=st[:, 0:nel]: e.tensor_copy(out=o, in_=i),
                  reads=[stres], writes=[sres])
        K.flush_stores(1)
        K.pending_stores.append(lambda: pg.op(STQ, lambda e, o=scr[idx][:, 0:nel], i=slot[:, 0:nel]: e.dma_start(out=o, in_=i),
                                              reads=[sres], writes=[scr_res[idx]], chan=K.cst_ch[si % 2]))


class Kern:
    def __init__(self, S=4096, upto="full"):
        self.S = S
        self.NT = S // T
        self.upto = upto
        self.nc = bass.Bass("TRN2", target_bir_lowering=False)
        self.pg = Prog(self.nc)
        self.convA = set()
        self.convB = set()
        self.stage_i = 0
        self.pending_stores = []
        self.qstores = []
        self.pend_b = None
        self.qk_i = 0
        self.build()

    def srcA(self, idx):
        if idx < 88:
            li, j = divmod(idx, 22)
            return self.d_win[li], j * 128, FF + j * 128
        if idx < 96:
            c = idx - 88
            return self.d_pw1, c * 128, D + c * 128
        if idx < 120:
            g, hh = divmod(idx - 96, 8)
            return self.d_qkv, g * 3072 + hh * 128, g * 3072 + 1024 + hh * 128
        g, pr = divmod(idx - 120, 4)
        c0 = g * 3072 + 2048 + pr * 256
        return self.d_qkv, c0, c0 + 128

    def srcB(self, idx):
        if idx < 32:
            li, dc = divmod(idx, 8)
            return self.d_wout[li], FC, dc * 128
        if idx < 40:
            return self.d_pw2, DC, (idx - 32) * 128
        return self.d_wo, DC, (idx - 40) * 128

    def flush_stores(self, keep):
        while len(self.pending_stores) > keep:
            self.pending_stores.pop(0)()

    def bank(self):
        b = self.banks[self.bank_i % 7]
        self.bank_i += 1
        return b

    def dram_in(self, name, shape, dt=F32):
        return self.nc.dram_tensor(name, list(shape), dt, kind="ExternalInput").ap()

    def build(self):
        nc, pg, S, NT = self.nc, self.pg, self.S, self.NT
        op = pg.op
        self.d_x = self.dram_in("x", [S, D])
        d_vrow = self.dram_in("vrow", [NVR, D])
        d_qkn = self.dram_in("qkn", [6, 128])
        d_ident = self.dram_in("ident", [P, P])
        d_mask = self.dram_in("maskneg", [P, 256])
        win = self.dram_in("ffn_w_in", [4, D, 2 * FF])
        wout = self.dram_in("ffn_w_out", [4, FF, D])
        self.d_win = [win[i] for i in range(4)]
        self.d_wout = [wout[i] for i in range(4)]
        self.d_pw1 = self.dram_in("conv_w_pw1", [D, 2 * D])
        self.d_pw2 = self.dram_in("conv_w_pw2", [D, D])
        self.d_qkv = self.dram_in("attn_w_qkv", [D, 9216])
        self.d_wo = self.dram_in("attn_w_o", [D, D])
        self.d_out = nc.dram_tensor("out", [S, D], F32, kind="ExternalOutput").ap()
        wA = nc.dram_tensor("wA", [NA, P, 2048], BF16).ap()
        wB = nc.dram_tensor("wB", [NB, P, FF], BF16).ap()
        self.wA = [wA[i] for i in range(NA)]
        self.wB = [wB[i] for i in range(NB)]
        self.wA_res = mkres(NA)
        self.wB_res = mkres(NB)
        self.xs = nc.dram_tensor("xs", [NT, P, DC * T], F32).ap()
        self.xs_res = mkres(NT)
        self.qk_s = nc.dram_tensor("qk_s", [3, 2, NHEAD, P, S], BF16).ap()
        self.v_s = nc.dram_tensor("v_s", [3, S, D], BF16).ap()
        self.ao_s = nc.dram_tensor("ao_s", [NHEAD, P, S], BF16).ap()
        self.att_res = Res()
        self.ao_res = Res()

        with contextlib.ExitStack() as top:
            sb = lambda name, shape, dt: top.enter_context(nc.sbuf_tensor(name, list(shape), dt))
            self.banks = [Bank(top.enter_context(nc.psum_tensor(f"bank{i}", [P, 512], F32))) for i in range(8)]
            self.bank_i = 0
            self.ident_f = sb("ident_f", [P, P], F32)
            self.ident_b = sb("ident_b", [P, P], BF16)
            self.ones_b = sb("ones_b", [P, P], BF16)
            self.mask_b = sb("mask_b", [P, 256], BF16)
            self.vcol = sb("vcol", [P, DC, NVR], F32)
            self.g32 = sb("g32", [P, DC, 7], F32)
            self.qkcol = sb("qkcol", [P, 6], F32)
            self.epsc = sb("epsc", [P, 2], F32)
            self.c_res = Res()
            self.stage_ch = [pg.chan(f"stg{i}") for i in range(3)]
            self.cst_ch = [pg.chan(f"cst{i}") for i in range(2)]
            self.ch_misc = pg.chan("misc")

            with contextlib.ExitStack() as es:
                sb1 = lambda name, shape, dt: es.enter_context(nc.sbuf_tensor(name, list(shape), dt))
                vrow = sb1("vrow_t", [NVR, D], F32)
                qkrow = sb1("qkrow_t", [6, 128], F32)
                maskf = sb1("maskf", [P, 256], F32)
                r_v, r_q, r_m, r_i = Res(), Res(), Res(), Res()
                chs = [pg.chan(f"su{i}") for i in range(4)]
                op("sp", lambda e: e.dma_start(out=vrow[:, :], in_=d_vrow), writes=[r_v], chan=chs[0])
                op("sp", lambda e: e.dma_start(out=qkrow[:, :], in_=d_qkn), writes=[r_q], chan=chs[1])
                op("sp", lambda e: e.dma_start(out=maskf[:, :], in_=d_mask), writes=[r_m], chan=chs[2])
                op("sp", lambda e: e.dma_start(out=self.ident_f[:, :], in_=d_ident), writes=[r_i], chan=chs[3])
                op("pool", lambda e: e.tensor_copy(out=self.ident_b[:, :], in_=self.ident_f[:, :]),
                   reads=[r_i], writes=[self.c_res])
                op("pool", lambda e: e.tensor_copy(out=self.mask_b[:, :], in_=maskf[:, :]),
                   reads=[r_m], writes=[self.c_res])
                op("pool", lambda e: e.memset(self.ones_b[:, :], 1.0), writes=[self.c_res])
                op("pool", lambda e: e.memset(self.epsc[:, 0:1], float(D * EPS)), writes=[self.c_res])
                op("pool", lambda e: e.memset(self.epsc[:, 1:2], float(128 * EPS)), writes=[self.c_res])
                for dc in range(DC):
                    b = self.bank()
                    op("pe", lambda e, b=b, dc=dc: e.transpose(out=b.t[:, 0:NVR], in_=vrow[:, dc * 128:(dc + 1) * 128],
                                                               identity=self.ident_f[0:NVR, 0:NVR]),
                       reads=[r_v, r_i], writes=[b.res])
                    op("dve", lambda e, b=b, dc=dc: e.tensor_copy(out=self.vcol[:, dc, :], in_=b.t[:, 0:NVR]),
                       reads=[b.res], writes=[self.c_res])
                b = self.bank()
                op("pe", lambda e, b=b: e.transpose(out=b.t[:, 0:6], in_=qkrow[:, :], identity=self.ident_f[0:6, 0:6]),
                   reads=[r_q, r_i], writes=[b.res])
                op("dve", lambda e, b=b: e.tensor_copy(out=self.qkcol[:, :], in_=b.t[:, 0:6]),
                   reads=[b.res], writes=[self.c_res])
                op("dve", lambda e: e.tensor_scalar(out=self.g32[:, :, 0:6], in0=self.vcol[:, :, 0:6], scalar1=float(np.sqrt(D)),
                                                    scalar2=None, op0=ALU.mult), reads=[self.c_res], writes=[self.c_res])
                op("dve", lambda e: e.tensor_scalar(out=self.g32[:, :, 6:7], in0=self.vcol[:, :, 9:10], scalar1=float(np.sqrt(D)),
                                                    scalar2=None, op0=ALU.mult), reads=[self.c_res], writes=[self.c_res])
                pg.barrier()
                pg.emit()

            upto = self.upto
            order = ["load", "ffn0a", "conv", "ffn0b", "ffn1a", "qkv", "full"]
            lvl = order.index(upto)
            with contextlib.ExitStack() as es:
                self.sbp = lambda name, shape, dt: es.enter_context(nc.sbuf_tensor("p1_" + name, list(shape), dt))
                self.alloc_common()
                sbp = self.sbp
                self.x_tm = sbp("x_tm", [P, 4, D], F32)
                self.xtm_res = Res()
                self.ch_xin = pg.chan("xin")
                self.ch_xout = pg.chan("xout")
                self.u_ext = sbp("u_ext", [P, DC, CW - 1 + T], BF16)
                self.u_res = Res()
                self.dg = [sbp(f"dg{i}", [P, CW, P], BF16) for i in range(2)]
                self.dg_res = mkres(2)
                self.sqh = [sbp(f"sqh{i}", [P, T], BF16) for i in range(3)]
                self.sqh_res = mkres(3)
                self.r2 = [sbp(f"r2_{i}", [P, T], F32) for i in range(3)]
                self.r2_res = mkres(3)
                self.qst = [sbp(f"qst{i}", [P, T], BF16) for i in range(3)]
                self.qst_res = mkres(3)
                self.qst_ch = [pg.chan(f"qst{i}") for i in range(3)]
                self.qst_i = 0
                self.vst = sbp("vst", [P, 4, D], BF16)
                self.vst_res = Res()
                self.vst_ch = pg.chan("vst")
                planA, planB = [], []
                for i in range(NT):
                    if lvl >= 1:
                        planA += list(range(0, 22)); planB += list(range(0, 8))
                    if lvl >= 2:
                        planA += list(range(88, 96)); planB += list(range(32, 40))
                    if lvl >= 3:
                        planA += list(range(22, 44)); planB += list(range(8, 16))
                    if lvl >= 4:
                        planA += list(range(44, 66)); planB += list(range(16, 24))
                    if lvl >= 5:
                        planA += list(range(96, 132))
                self.wsA = WStream(self, "A", self.slotsA, planA)
                self.wsB = WStream(self, "B", self.slotsB, planB)
                for i in range(NT):
                    self.load_x(i)
                    if lvl >= 1:
                        self.ffn(0, 0, 0)
                    if lvl >= 2:
                        self.conv(i)
                    if lvl >= 3:
                        self.ffn(1, 2, 8)
                    if lvl >= 4:
                        self.ffn(2, 3, 16)
                    if lvl >= 5:
                        self.qkv(i)
                        op(STQ, lambda e, i=i: e.dma_start(out=self.xs[i], in_=self.xT[:, :, :].rearrange("p a t -> p (a t)")),
                           reads=self.xT_res, writes=[self.xs_res[i]], chan=self.ch_xout)
                    else:
                        self.store_out(i, self.x_tm, self.xtm_res)
                self.flush_stores(0)
                pg.barrier()
                pg.emit()
            if lvl < 5:
                return
            with contextlib.ExitStack() as es:
                self.sbp = lambda name, shape, dt: es.enter_context(nc.sbuf_tensor("p2_" + name, list(shape), dt))
                self.attention()
                pg.barrier()
                pg.emit()
            with contextlib.ExitStack() as es:
                self.sbp = lambda name, shape, dt: es.enter_context(nc.sbuf_tensor("p3_" + name, list(shape), dt))
                self.alloc_common()
                sbp = self.sbp
                self.otm = sbp("otm", [P, 4, D], F32)
                self.otm_res = Res()
                self.ch_xin = pg.chan("xin3")
                self.ch_xout = pg.chan("xout3")
                aot = sbp("aot", [P, NHEAD, T], BF16)
                aot_res = Res()
                ch_ao = pg.chan("aoin")
                planA, planB = [], []
                for i in range(NT):
                    planB += list(range(40, 48))
                    planA += list(range(66, 88)); planB += list(range(24, 32))
                self.wsA = WStream(self, "A", self.slotsA, planA)
                self.wsB = WStream(self, "B", self.slotsB, planB)
                for i in range(NT):
                    op("sp", lambda e, i=i: e.dma_start(out=self.xT[:, :, :].rearrange("p a t -> p (a t)"), in_=self.xs[i]),
                       reads=[self.xs_res[i]], writes=self.xT_res, chan=self.ch_xin)
                    op("sp", lambda e, i=i: e.dma_start(out=aot[:, :, :],
                                                        in_=self.ao_s[:, :, i * T:(i + 1) * T].rearrange("h e t -> e h t")),
                       reads=[self.ao_res], writes=[aot_res], chan=ch_ao)
                    pend = None
                    for dc in range(DC):
                        slot, sres = self.wsB.acquire()
                        sv = slot[:, 0:DC * 128].rearrange("p (a c) -> p a c", c=128)
                        b = self.bank()
                        for hh in range(NHEAD):
                            op("pe", lambda e, b=b, sv=sv, hh=hh: e.matmul(b.t[:, :], sv[:, hh, :], aot[:, hh, :],
                                                                           start=(hh == 0), stop=(hh == NHEAD - 1)),
                               reads=[sres, aot_res], writes=[b.res], inc=(hh == NHEAD - 1))
                        if pend is not None:
                            pend()
                            pend = None
                        op("dve", lambda e, b=b, dc=dc: e.tensor_tensor(out=self.xT[:, dc, :], in0=b.t[:, :], in1=self.xT[:, dc, :],
                                                                        op=ALU.add),
                           reads=[b.res, self.xT_res[dc]], writes=[self.xT_res[dc]])
                        pend = self.stat_chunk(self.xT[:, dc, :], [self.xT_res[dc]], dc)
                    pend()
                    self.ffn(3, 5, 24, stats=False)
                    self.store_out(i, self.otm, self.otm_res)
                self.flush_stores(0)
                pg.barrier()
                pg.emit()

    def alloc_common(self):
        sbp = self.sbp
        self.stageF = [sbp(f"stageF{i}", [P, FF], F32) for i in range(3)]
        self.stage_res = mkres(3)
        self.slotsA = [sbp(f"slotA{i}", [P, 2048], BF16) for i in range(6)]
        self.slotsB = [sbp(f"slotB{i}", [P, FF], BF16) for i in range(3)]
        self.xT = sbp("xT", [P, DC, T], F32)
        self.xT_res = mkres(DC)
        self.sq = sbp("sq", [P, DC, T], BF16)
        self.sq_res = mkres(DC)
        self.h = sbp("h", [P, DC, T], BF16)
        self.h_res = mkres(DC)
        self.rstd = sbp("rstd", [P, T], F32)
        self.rstd_res = Res()
        self.Gflat = sbp("G", [P, FC * T], BF16)
        self.G = self.Gflat[:, :].rearrange("p (j t) -> p j t", t=T)
        self.cvf = self.Gflat[:, 0:16 * T].bitcast(F32).rearrange("p (c t) -> p c t", t=T)
        self.G_res = mkres(FC)
        self.sg = [sbp(f"sg{i}", [P, T], F32) for i in range(2)]
        self.sg_res = mkres(2)
        self.sg_i = 0

    def load_x(self, i):
        op = self.pg.op
        op("sp", lambda e: e.dma_start(out=self.x_tm[:, :, :],
                                       in_=self.d_x[i * T:(i + 1) * T, :].rearrange("(b p) d -> p b d", p=P)),
           writes=[self.xtm_res], chan=self.ch_xin)
        pend = None
        for dc in range(DC):
            b = self.bank()
            for tb in range(4):
                op("pe", lambda e, b=b, dc=dc, tb=tb: e.transpose(out=b.t[:, tb * 128:(tb + 1) * 128],
                                                                  in_=self.x_tm[:, tb, dc * 128:(dc + 1) * 128],
                                                                  identity=self.ident_f[:, :]),
                   reads=[self.xtm_res], writes=[b.res], inc=(tb == 3))
            if pend is not None:
                pend()
                pend = None
            if dc % 2 == 0:
                op("dve", lambda e, b=b, dc=dc: e.tensor_copy(out=self.xT[:, dc, :], in_=b.t[:, :]),
                   reads=[b.res], writes=[self.xT_res[dc]])
            else:
                op("act", lambda e, b=b, dc=dc: e.activation(out=self.xT[:, dc, :], in_=b.t[:, :], func=AF.Identity),
                   reads=[b.res], writes=[self.xT_res[dc]])
            pend = self.stat_chunk(self.xT[:, dc, :], [self.xT_res[dc]], dc)
        if pend is not None:
            pend()

    def store_out(self, i, otm, otm_res):
        op = self.pg.op
        for tb in range(4):
            for half in range(2):
                b = self.bank()
                for k in range(4):
                    dc = half * 4 + k
                    op("pe", lambda e, b=b, dc=dc, tb=tb, k=k: e.transpose(out=b.t[:, k * 128:(k + 1) * 128],
                                                                           in_=self.xT[:, dc, tb * 128:(tb + 1) * 128],
                                                                           identity=self.ident_f[:, :]),
                       reads=[self.xT_res[dc]], writes=[b.res], inc=(k == 3))
                if half == 0:
                    op("dve", lambda e, b=b, tb=tb: e.tensor_copy(out=otm[:, tb, 0:512], in_=b.t[:, :]),
                       reads=[b.res], writes=[otm_res])
                else:
                    op("act", lambda e, b=b, tb=tb: e.activation(out=otm[:, tb, 512:1024], in_=b.t[:, :], func=AF.Identity),
                       reads=[b.res], writes=[otm_res])
        op(STQ, lambda e: e.dma_start(out=self.d_out[i * T:(i + 1) * T, :].rearrange("(b p) d -> p b d", p=P),
                                        in_=otm[:, :, :]),
           reads=[otm_res], writes=[], chan=self.ch_xout)

    def stat_chunk(self, src, src_res, dc):
        op = self.pg.op
        op("act", lambda e: e.activation(out=self.sq[:, dc, :], in_=src, func=AF.Square),
           reads=src_res, writes=[self.sq_res[dc]])
        b = self.banks[7]

        def pe():
            op("pe", lambda e: e.matmul(b.t[:, :], self.ones_b[:, :], self.sq[:, dc, :], start=(dc == 0), stop=(dc == DC - 1)),
               reads=[self.sq_res[dc], self.c_res], writes=[b.res], inc=(dc == DC - 1))
        return pe

    def rsqrt(self, b, dst, dst_res, epsname):
        op = self.pg.op
        col = 0 if epsname == "epsD" else 1
        op("act", lambda e: e.activation(out=dst[:, :], in_=b.t[:, :], func=AF.Ln, bias=self.epsc[:, col:col + 1]),
           reads=[b.res, self.c_res], writes=[dst_res])
        op("act", lambda e: e.activation(out=dst[:, :], in_=dst[:, :], func=AF.Exp, scale=-0.5),
           reads=[dst_res], writes=[dst_res])

    def norm_h(self, gi):
        op = self.pg.op
        self.rsqrt(self.banks[7], self.rstd, self.rstd_res, "epsD")
        for dc in range(DC):
            op("dve", lambda e, dc=dc: e.scalar_tensor_tensor(out=self.h[:, dc, :], in0=self.xT[:, dc, :],
                                                              scalar=self.g32[:, dc, gi:gi + 1], in1=self.rstd[:, :],
                                                              op0=ALU.mult, op1=ALU.mult),
               reads=[self.xT_res[dc], self.rstd_res, self.c_res], writes=[self.h_res[dc]])

    def ffn(self, li, gi, b0, stats=True):
        op = self.pg.op
        self.norm_h(gi)
        for j in range(FC):
            slot, sres = self.wsA.acquire()
            sv = slot[:, :].rearrange("p (a c) -> p a c", c=256)
            bg, bu = self.bank(), self.bank()
            for half, b in ((0, bg), (1, bu)):
                for dc in range(DC):
                    op("pe", lambda e, b=b, sv=sv, dc=dc, half=half: e.matmul(b.t[:, :], sv[:, dc, half * 128:(half + 1) * 128],
                                                                              self.h[:, dc, :], start=(dc == 0), stop=(dc == DC - 1)),
                       reads=[sres, self.h_res[dc]], writes=[b.res], inc=(dc == DC - 1))
            k = self.sg_i % 2
            self.sg_i += 1
            op("act", lambda e, k=k, bg=bg: e.activation(out=self.sg[k][:, :], in_=bg.t[:, :], func=AF.Silu),
               reads=[bg.res], writes=[self.sg_res[k]])
            op("dve", lambda e, k=k, bu=bu, j=j: e.tensor_tensor(out=self.G[:, j, :], in0=bu.t[:, :], in1=self.sg[k][:, :], op=ALU.mult),
               reads=[bu.res, self.sg_res[k]], writes=[self.G_res[j]])
        pend = None
        for dc in range(DC):
            slot, sres = self.wsB.acquire()
            sv = slot[:, :].rearrange("p (a c) -> p a c", c=128)
            b = self.bank()
            for j in range(FC):
                op("pe", lambda e, b=b, sv=sv, j=j: e.matmul(b.t[:, :], sv[:, j, :], self.G[:, j, :], start=(j == 0), stop=(j == FC - 1)),
                   reads=[sres, self.G_res[j]], writes=[b.res], inc=(j == FC - 1))
            if pend is not None:
                pend()
                pend = None
            op("dve", lambda e, b=b, dc=dc: e.scalar_tensor_tensor(out=self.xT[:, dc, :], in0=b.t[:, :], scalar=0.5,
                                                                   in1=self.xT[:, dc, :], op0=ALU.mult, op1=ALU.add),
               reads=[b.res, self.xT_res[dc]], writes=[self.xT_res[dc]])
            if stats:
                pend = self.stat_chunk(self.xT[:, dc, :], [self.xT_res[dc]], dc)
        if pend is not None:
            pend()

    def conv(self, i):
        op = self.pg.op
        HL = CW - 1
        self.norm_h(1)
        if i == 0:
            op("pool", lambda e: e.memset(self.u_ext[:, :, 0:HL], 0.0), writes=[self.u_res])
        else:
            op("pool", lambda e: e.tensor_copy(out=self.u_ext[:, :, 0:HL], in_=self.u_ext[:, :, T:T + HL]),
               reads=[self.u_res], writes=[self.u_res])
        for c in range(DC):
            slot, sres = self.wsA.acquire()
            sv = slot[:, :].rearrange("p (a c) -> p a c", c=256)
            ba, bgt = self.bank(), self.bank()
            for half, b in ((0, ba), (1, bgt)):
                for dc in range(DC):
                    op("pe", lambda e, b=b, sv=sv, dc=dc, half=half: e.matmul(b.t[:, :], sv[:, dc, half * 128:(half + 1) * 128],
                                                                              self.h[:, dc, :], start=(dc == 0), stop=(dc == DC - 1)),
                       reads=[sres, self.h_res[dc]], writes=[b.res], inc=(dc == DC - 1))
            k = self.sg_i % 2
            self.sg_i += 1
            op("act", lambda e, k=k, b=bgt, c=c: e.activation(out=self.sg[k][:, :], in_=b.t[:, :], func=AF.Sigmoid,
                                                              bias=self.vcol[:, c, 7:8]),
               reads=[bgt.res, self.c_res], writes=[self.sg_res[k]])
            op("dve", lambda e, k=k, b=ba, c=c: e.scalar_tensor_tensor(out=self.u_ext[:, c, HL:HL + T], in0=b.t[:, :],
                                                                       scalar=self.vcol[:, c, 6:7], in1=self.sg[k][:, :],
                                                                       op0=ALU.add, op1=ALU.mult),
               reads=[ba.res, self.sg_res[k], self.c_res], writes=[self.u_res])
        cvf = self.cvf
        cres = lambda c: [self.G_res[2 * c], self.G_res[2 * c + 1]]
        pend = None
        for c in range(DC):
            k = c % 2
            op("pool", lambda e, k=k, c=c: e.tensor_tensor(out=self.dg[k][:, :, :],
                                                           in0=self.ident_b[:, :].unsqueeze(1).to_broadcast([P, CW, P]),
                                                           in1=self.vcol[:, c, 11:11 + CW].unsqueeze(2).to_broadcast([P, CW, P]),
                                                           op=ALU.mult),
               reads=[self.c_res], writes=[self.dg_res[k]])
            b = self.bank()
            for j in range(CW):
                op("pe", lambda e, b=b, k=k, c=c, j=j: e.matmul(b.t[:, :], self.dg[k][:, j, :], self.u_ext[:, c, j:j + T],
                                                                start=(j == 0), stop=(j == CW - 1)),
                   reads=[self.dg_res[k], self.u_res], writes=[b.res], inc=(j == CW - 1))
            if pend is not None:
                pend()
                pend = None
            op("act", lambda e, b=b, c=c: e.activation(out=cvf[:, c, :], in_=b.t[:, :], func=AF.Identity, bias=self.vcol[:, c, 8:9]),
               reads=[b.res, self.c_res], writes=cres(c))
            pend = self.stat_chunk(cvf[:, c, :], cres(c), c)
        if pend is not None:
            pend()
            pend = None
        self.rsqrt(self.banks[7], self.rstd, self.rstd_res, "epsD")
        for c in range(DC):
            op("dve", lambda e, c=c: e.scalar_tensor_tensor(out=cvf[:, c, :], in0=cvf[:, c, :], scalar=self.g32[:, c, 6:7],
                                                            in1=self.rstd[:, :], op0=ALU.mult, op1=ALU.mult),
               reads=cres(c) + [self.rstd_res, self.c_res], writes=cres(c))
            op("act", lambda e, c=c: e.activation(out=self.h[:, c, :], in_=cvf[:, c, :], func=AF.Silu),
               reads=cres(c), writes=[self.h_res[c]])
        for dc in range(DC):
            slot, sres = self.wsB.acquire()
            sv = slot[:, 0:DC * 128].rearrange("p (a c) -> p a c", c=128)
            b = self.bank()
            for c in range(DC):
                op("pe", lambda e, b=b, sv=sv, c=c: e.matmul(b.t[:, :], sv[:, c, :], self.h[:, c, :], start=(c == 0), stop=(c == DC - 1)),
                   reads=[sres, self.h_res[c]], writes=[b.res], inc=(c == DC - 1))
            if pend is not None:
                pend()
                pend = None
            op("dve", lambda e, b=b, dc=dc: e.scalar_tensor_tensor(out=self.xT[:, dc, :], in0=b.t[:, :], scalar=self.vcol[:, dc, 10:11],
                                                                   in1=self.xT[:, dc, :], op0=ALU.add, op1=ALU.add),
               reads=[b.res, self.xT_res[dc], self.c_res], writes=[self.xT_res[dc]])
            pend = self.stat_chunk(self.xT[:, dc, :], [self.xT_res[dc]], dc)
        if pend is not None:
            pend()

    def qkv(self, i):
        op = self.pg.op
        self.norm_h(4)
        pend = None
        for g in range(3):
            for hh in range(NHEAD):
                slot, sres = self.wsA.acquire()
                sv = slot[:, :].rearrange("p (a c) -> p a c", c=256)
                for which in range(2):
                    b1 = self.bank()
                    for dc in range(DC):
                        op("pe", lambda e, b=b1, sv=sv, dc=dc, which=which: e.matmul(b.t[:, :], sv[:, dc, which * 128:(which + 1) * 128],
                                                                                     self.h[:, dc, :], start=(dc == 0), stop=(dc == DC - 1)),
                           reads=[sres, self.h_res[dc]], writes=[b1.res], inc=(dc == DC - 1))
                    k = self.qk_i % 3
                    self.qk_i += 1
                    op("act", lambda e, k=k, b=b1: e.activation(out=self.sqh[k][:, :], in_=b.t[:, :], func=AF.Square),
                       reads=[b1.res], writes=[self.sqh_res[k]])
                    if pend is not None:
                        pend()
                        pend = None

                    def tail(k=k, b1=b1, g=g, which=which, hh=hh):
                        b2 = self.bank()
                        op("pe", lambda e: e.matmul(b2.t[:, :], self.ones_b[:, :], self.sqh[k][:, :], start=True, stop=True),
                           reads=[self.sqh_res[k], self.c_res], writes=[b2.res])
                        r2, r2r = self.r2[k], self.r2_res[k]
                        op("act", lambda e: e.activation(out=r2[:, :], in_=b2.t[:, :], func=AF.Ln, bias=self.epsc[:, 1:2]),
                           reads=[b2.res, self.c_res], writes=[r2r])

                        def stage_b():
                            op("act", lambda e: e.activation(out=r2[:, :], in_=r2[:, :], func=AF.Exp, scale=-0.5),
                               reads=[r2r], writes=[r2r])
                            qi = self.qst_i % 3
                            self.qst_i += 1
                            col = which * 3 + g
                            op("dve", lambda e: e.scalar_tensor_tensor(
                                out=self.qst[qi][:, :], in0=b1.t[:, :], scalar=self.qkcol[:, col:col + 1], in1=r2[:, :],
                                op0=ALU.mult, op1=ALU.mult),
                               reads=[b1.res, r2r, self.c_res], writes=[self.qst_res[qi]])
                            while self.qstores:
                                self.qstores.pop(0)()
                            self.qstores.append(lambda: op(
                                STQ, lambda e: e.dma_start(out=self.qk_s[g, which, hh, :, i * T:(i + 1) * T], in_=self.qst[qi][:, :]),
                                reads=[self.qst_res[qi]], writes=[self.att_res], chan=self.qst_ch[qi]))
                        prevb = self.pend_b
                        self.pend_b = stage_b
                        if prevb is not None:
                            prevb()
                    pend = tail
        if pend is not None:
            pend()
        if self.pend_b is not None:
            self.pend_b()
            self.pend_b = None
        while self.qstores:
            self.qstores.pop(0)()
        cnt = 0
        for g in range(3):
            for pr in range(4):
                slot, sres = self.wsA.acquire()
                sv = slot[:, :].rearrange("p (a c) -> p a c", c=256)
                for t2 in range(2):
                    b = self.bank()
                    for q in range(2):
                        tb = t2 * 2 + q
                        for dc in range(DC):
                            op("pe", lambda e, b=b, sv=sv, dc=dc, tb=tb, q=q: e.matmul(
                                b.t[:, q * 256:(q + 1) * 256], self.h[:, dc, tb * 128:(tb + 1) * 128], sv[:, dc, :],
                                start=(dc == 0), stop=(dc == DC - 1)),
                               reads=[sres, self.h_res[dc]], writes=[b.res], inc=(dc == DC - 1 and q == 1))
                    outv = self.vst[:, t2 * 2:t2 * 2 + 2, pr * 256:(pr + 1) * 256]
                    inv = b.t[:, :].rearrange("p (q c) -> p q c", c=256)
                    if cnt % 2 == 0:
                        op("dve", lambda e, outv=outv, inv=inv: e.tensor_copy(out=outv, in_=inv), reads=[b.res], writes=[self.vst_res])
                    else:
                        op("act", lambda e, outv=outv, inv=inv: e.activation(out=outv, in_=inv, func=AF.Identity),
                           reads=[b.res], writes=[self.vst_res])
                    cnt += 1
            op(STQ, lambda e, g=g: e.dma_start(out=self.v_s[g, i * T:(i + 1) * T, :].rearrange("(b p) c -> p b c", p=P),
                                                 in_=self.vst[:, :, :]),
               reads=[self.vst_res], writes=[self.att_res], chan=self.vst_ch)

    def attention(self):
        nc, pg, S = self.nc, self.pg, self.S
        op = pg.op
        sbp = self.sbp
        NBLK = S // 128
        qraws = [sbp(f"qraw{i}", [P, S], BF16) for i in range(2)]
        kraws = [sbp(f"kraw{i}", [P, S], BF16) for i in range(2)]
        qraw_ress, kraw_ress = mkres(2), mkres(2)
        ch_qrs = [pg.chan(f"qraw{i}") for i in range(2)]
        ch_krs = [pg.chan(f"kraw{i}") for i in range(2)]
        raw_i = [0]
        NBUF = 3
        qs = [sbp(f"qs{i}", [P, S], BF16) for i in range(NBUF)]
        ks = [sbp(f"ks{i}", [P, S], BF16) for i in range(NBUF)]
        vs = [sbp(f"vs{i}", [P, NBLK, P], BF16) for i in range(NBUF)]
        qs_res, ks_res, vs_res = mkres(NBUF), mkres(NBUF), mkres(NBUF)
        ch_q = [pg.chan(f"qs{i}") for i in range(NBUF)]
        ch_k = [pg.chan(f"ks{i}") for i in range(NBUF)]
        ch_v = [pg.chan(f"vs{i}") for i in range(NBUF)]
        Pt = sbp("Pt", [P, NBLK, 256], BF16)
        Pt_res = mkres(NBLK)
        oaccs = [sbp(f"oacc{i}", [P, S], F32) for i in range(2)]
        laccs = [sbp(f"lacc{i}", [P, S], F32) for i in range(2)]
        oacc_ress, lacc_ress = mkres(2), mkres(2)
        pend_ao = None
        aost = sbp("aost", [P, S], BF16)
        aost_res = Res()
        ch_ao = pg.chan("aost")
        SC = float(np.sqrt(128.0))
        iters = [(hh, g) for hh in range(NHEAD) for g in range(3)]

        def loads(it):
            hh, g = iters[it]
            d = DIL[g]
            nb = (S // d) // 128
            bi = it % NBUF
            if d == 1:
                op("sp", lambda e, bi=bi, g=g, hh=hh: e.dma_start(out=qs[bi][:, :], in_=self.qk_s[g, 0, hh]),
                   reads=[self.att_res], writes=[qs_res[bi]], chan=ch_q[bi])
                op("sp", lambda e, bi=bi, g=g, hh=hh: e.dma_start(out=ks[bi][:, :], in_=self.qk_s[g, 1, hh]),
                   reads=[self.att_res], writes=[ks_res[bi]], chan=ch_k[bi])
            else:
                ri = raw_i[0] % 2
                raw_i[0] += 1
                qraw, kraw = qraws[ri], kraws[ri]
                qraw_res, kraw_res = qraw_ress[ri], kraw_ress[ri]
                ch_qr, ch_kr = ch_qrs[ri], ch_krs[ri]
                op("sp", lambda e, g=g, hh=hh: e.dma_start(out=qraw[:, :], in_=self.qk_s[g, 0, hh]),
                   reads=[self.att_res], writes=[qraw_res], chan=ch_qr)
                op("sp", lambda e, g=g, hh=hh: e.dma_start(out=kraw[:, :], in_=self.qk_s[g, 1, hh]),
                   reads=[self.att_res], writes=[kraw_res], chan=ch_kr)
                op("pool", lambda e, bi=bi, d=d: e.tensor_copy(out=qs[bi][:, :].rearrange("p (r m) -> p r m", r=d),
                                                               in_=qraw[:, :].rearrange("p (m r) -> p r m", r=d)),
                   reads=[qraw_res], writes=[qs_res[bi]])
                op("pool", lambda e, bi=bi, d=d: e.tensor_copy(out=ks[bi][:, :].rearrange("p (r m) -> p r m", r=d),
                                                               in_=kraw[:, :].rearrange("p (m r) -> p r m", r=d)),
                   reads=[kraw_res], writes=[ks_res[bi]])
            for r in range(d):
                src = self.v_s[g, :, hh * 128:(hh + 1) * 128].rearrange("(kb p r) e -> r p kb e", p=P, r=d)[r]
                op("sp", lambda e, bi=bi, src=src, r=r, nb=nb: e.dma_start(out=vs[bi][:, r * nb:(r + 1) * nb, :], in_=src),
                   reads=[self.att_res], writes=[vs_res[bi]], chan=ch_v[bi], chain=(r > 0))

        loads(0)
        loads(1)
        it = 0
        for hh in range(NHEAD):
            oacc, lacc = oaccs[hh % 2], laccs[hh % 2]
            oacc_res, lacc_res = oacc_ress[hh % 2], lacc_ress[hh % 2]
            for g in range(3):
                d = DIL[g]
                L = S // d
                nb = L // 128
                bi = it % NBUF
                if it + 2 < len(iters):
                    loads(it + 2)
                it += 1
                def scores(r, n0):
                    base = r * L
                    for kb in range(n0, min(n0 + 4, nb)):
                        nq = 2 if kb < nb - 1 else 1
                        blk = r * nb + kb
                        b = self.bank()
                        op("pe", lambda e, b=b, bi=bi, base=base, kb=kb, nq=nq: e.matmul(
                            b.t[:, 0:128 * nq], ks[bi][:, base + kb * 128:base + (kb + 1) * 128],
                            qs[bi][:, base + kb * 128:base + (kb + nq) * 128], start=True, stop=False),
                           reads=[ks_res[bi], qs_res[bi]], writes=[b.res], inc=False)
                        op("pe", lambda e, b=b, nq=nq: e.matmul(b.t[:, 0:128 * nq], self.ident_b[:, :], self.mask_b[:, 0:128 * nq],
                                                                start=False, stop=True),
                           reads=[self.c_res], writes=[b.res])
                        op("act", lambda e, b=b, blk=blk, nq=nq: e.activation(out=Pt[:, blk, 0:128 * nq], in_=b.t[:, 0:128 * nq],
                                                                             func=AF.Exp, scale=SC),
                           reads=[b.res], writes=[Pt_res[blk]])
                def pv(r, n0):
                    nn = min(4, nb - n0)
                    bo, bl = self.bank(), self.bank()
                    blk0 = r * nb + n0
                    for n in range(n0, n0 + nn):
                        blk = r * nb + n
                        o = bo.t[:, (n - n0) * 128:(n - n0 + 1) * 128]
                        last = (n == n0 + nn - 1)
                        if n > 0:
                            op("pe", lambda e, o=o, bi=bi, blk=blk: e.matmul(o, vs[bi][:, blk - 1, :], Pt[:, blk - 1, 128:256],
                                                                             start=True, stop=False),
                               reads=[vs_res[bi], Pt_res[blk - 1]], writes=[bo.res], inc=False)
                        op("pe", lambda e, o=o, bi=bi, blk=blk, n=n: e.matmul(o, vs[bi][:, blk, :], Pt[:, blk, 0:128],
                                                                              start=(n == 0), stop=True),
                           reads=[vs_res[bi], Pt_res[blk]], writes=[bo.res], inc=last)
                    if n0 == 0:
                        op("pe", lambda e, bl=bl, blk0=blk0: e.matmul(bl.t[:, 0:128], self.ones_b[:, :], Pt[:, blk0, 0:128],
                                                                      start=True, stop=True),
                           reads=[Pt_res[blk0], self.c_res], writes=[bl.res], inc=(nn == 1))
                        if nn > 1:
                            op("pe", lambda e, bl=bl, blk0=blk0, nn=nn: e.matmul(bl.t[:, 128:nn * 128], self.ones_b[:, :],
                                                                                 Pt[:, blk0 + 1:blk0 + nn, 0:128], start=True, stop=False),
                               reads=[Pt_res[blk0 + q] for q in range(1, nn)], writes=[bl.res], inc=False)
                            op("pe", lambda e, bl=bl, blk0=blk0, nn=nn: e.matmul(bl.t[:, 128:nn * 128], self.ones_b[:, :],
                                                                                 Pt[:, blk0:blk0 + nn - 1, 128:256], start=False, stop=True),
                               reads=[Pt_res[blk0 + q] for q in range(0, nn - 1)], writes=[bl.res])
                    else:
                        op("pe", lambda e, bl=bl, blk0=blk0, nn=nn: e.matmul(bl.t[:, 0:nn * 128], self.ones_b[:, :],
                                                                             Pt[:, blk0:blk0 + nn, 0:128], start=True, stop=False),
                           reads=[Pt_res[blk0 + q] for q in range(nn)], writes=[bl.res], inc=False)
                        op("pe", lambda e, bl=bl, blk0=blk0, nn=nn: e.matmul(bl.t[:, 0:nn * 128], self.ones_b[:, :],
                                                                             Pt[:, blk0 - 1:blk0 + nn - 1, 128:256], start=False, stop=True),
                           reads=[Pt_res[blk0 - 1 + q] for q in range(nn)], writes=[bl.res])
                    lo = n0 * 128 * d + r
                    cntq = nn * 128
                    ov = oacc[:, :].rearrange("p (m r) -> p r m", r=d)[:, r, n0 * 128:n0 * 128 + cntq]
                    lv = lacc[:, :].rearrange("p (m r) -> p r m", r=d)[:, r, n0 * 128:n0 * 128 + cntq]
                    if g == 0:
                        op("dve", lambda e, ov=ov, bo=bo, cntq=cntq: e.tensor_copy(out=ov, in_=bo.t[:, 0:cntq]),
                           reads=[bo.res], writes=[oacc_res])
                        op("act", lambda e, lv=lv, bl=bl, cntq=cntq: e.activation(out=lv, in_=bl.t[:, 0:cntq], func=AF.Identity),
                           reads=[bl.res], writes=[lacc_res])
                    else:
                        op("dve", lambda e, ov=ov, bo=bo, cntq=cntq: e.tensor_tensor(out=ov, in0=bo.t[:, 0:cntq], in1=ov, op=ALU.add),
                           reads=[bo.res, oacc_res], writes=[oacc_res])
                        op("dve", lambda e, lv=lv, bl=bl, cntq=cntq: e.tensor_tensor(out=lv, in0=bl.t[:, 0:cntq], in1=lv, op=ALU.add),
                           reads=[bl.res, lacc_res], writes=[lacc_res])

                units = [(r, n0) for r in range(d) for n0 in range(0, nb, 4)]
                scores(*units[0])
                for ui, u in enumerate(units):
                    if ui + 1 < len(units):
                        scores(*units[ui + 1])
                    pv(*u)
                if g == 0 and pend_ao is not None:
                    pend_ao()
                    pend_ao = None
            def fin(hh=hh, oacc=oacc, lacc=lacc, oacc_res=oacc_res, lacc_res=lacc_res):
                op("act", lambda e: e.activation(out=lacc[:, :], in_=lacc[:, :], func=AF.Ln), reads=[lacc_res], writes=[lacc_res])
                op("act", lambda e: e.activation(out=lacc[:, :], in_=lacc[:, :], func=AF.Exp, scale=-1.0),
                   reads=[lacc_res], writes=[lacc_res])
                op("dve", lambda e: e.tensor_tensor(out=aost[:, :], in0=oacc[:, :], in1=lacc[:, :], op=ALU.mult),
                   reads=[oacc_res, lacc_res], writes=[aost_res])
                op(STQ, lambda e: e.dma_start(out=self.ao_s[hh], in_=aost[:, :]),
                   reads=[aost_res], writes=[self.ao_res], chan=ch_ao)
            pend_ao = fin
        pend_ao()


_CACHE = {}


def host_consts():
    ident = np.eye(P, dtype=np.float32)
    j = np.arange(P)[:, None]
    i = np.arange(P)[None, :]
    cur = np.where(i >= j, 0.0, NEG)
    prev = np.where(j >= i, 0.0, NEG)
    mask = np.concatenate([cur, prev], axis=1).astype(np.float32)
    return ident, mask


def make_in_maps(inputs, n_cores=8, S=4096):
    f = lambda a: np.ascontiguousarray(np.asarray(a, dtype=np.float32))
    ident, mask = host_consts()
    vrow = np.concatenate([
        f(inputs["norm_g"]).reshape(6, D),
        f(inputs["conv_b_pw1"]).reshape(2, D),
        f(inputs["conv_b_dw"]).reshape(1, D),
        f(inputs["conv_norm_g"]).reshape(1, D),
        f(inputs["conv_b_pw2"]).reshape(1, D),
        f(inputs["conv_w_dw"]).reshape(CW, D),
    ], axis=0)
    qkn = np.concatenate([f(inputs["attn_q_norm"]).reshape(3, 128), f(inputs["attn_k_norm"]).reshape(3, 128)], axis=0)
    shared = {
        "vrow": f(vrow), "qkn": f(qkn), "ident": ident, "maskneg": mask,
        "ffn_w_in": f(inputs["ffn_w_in"]).reshape(4, D, 2 * FF),
        "ffn_w_out": f(inputs["ffn_w_out"]).reshape(4, FF, D),
        "conv_w_pw1": f(inputs["conv_w_pw1"]).reshape(D, 2 * D),
        "conv_w_pw2": f(inputs["conv_w_pw2"]).reshape(D, D),
        "attn_w_qkv": f(inputs["attn_w_qkv"]).reshape(D, 9216),
        "attn_w_o": f(inputs["attn_w_o"]).reshape(D, D),
    }
    x = f(inputs["x"])
    maps = []
    for c in range(n_cores):
        m = dict(shared)
        m["x"] = np.ascontiguousarray(x[c, :S])
        maps.append(m)
    return maps


def kernel(**inputs):
    if "k" not in _CACHE:
        _CACHE["k"] = Kern(S=4096, upto="full")
    k = _CACHE["k"]
    maps = make_in_maps(inputs, 8, 4096)
    res = run_bass_kernel_spmd(k.nc, maps, core_ids=list(range(8)))
    out = np.stack([np.asarray(r["out"], dtype=np.float32) for r in res.results], axis=0)
    return out
```
